# Optimizing a Trainium2 kernel written in Bass

```python
import math
import jax, jax.numpy as jnp
from jax import lax
import numpy as np

D_MODEL = 1024
BATCH = 16
SEQ = 2048
DEPTH = 1

N_META = 16
D_MIX = D_MODEL
C_CONV = D_MIX // 2
CONV_GROUPS = 8
CONV_WIDTH = 31
GLA_V = D_MIX - C_CONV
GLA_HEADS = 4
GLA_DV = GLA_V // GLA_HEADS
GLA_DK = GLA_DV // 2
GLA_K = GLA_HEADS * GLA_DK
GATE_RANK = 16
GATE_TAU = 16.0
CHUNK = 64
META_PAD = CHUNK - N_META
D_FF = int(math.ceil(D_MODEL * 8 / 3 / 256) * 256)
RMS_EPS = 1e-6
LN_EPS = 1e-5

IN_SPLITS = [C_CONV, C_CONV, GLA_K, GLA_K, GLA_V, GLA_V, GATE_RANK]
D_IN = sum(IN_SPLITS)

kernel_name = "hymba_conformer_conv_gla_hybrid"


def rms_norm(x, g):
    xf = x.astype(jnp.float32)
    y = xf * lax.rsqrt(jnp.mean(xf * xf, axis=-1, keepdims=True) + RMS_EPS)
    return (y * g.astype(jnp.float32)).astype(x.dtype)


def conv_group(u_val, u_gate, conv_w, conv_b, ln_g, ln_b):
    v = u_val * jax.nn.sigmoid(u_gate)
    y = lax.conv_general_dilated(
        v, conv_w[:, None, :].astype(v.dtype),
        window_strides=(1,), padding=((CONV_WIDTH - 1, 0),),
        dimension_numbers=("NWC", "WIO", "NWC"),
        feature_group_count=C_CONV) + conv_b
    yf = y.astype(jnp.float32)
    mu = jnp.mean(yf, axis=-1, keepdims=True)
    var = jnp.mean(jnp.square(yf - mu), axis=-1, keepdims=True)
    yf = (yf - mu) * lax.rsqrt(var + LN_EPS) * ln_g.astype(jnp.float32) + ln_b.astype(jnp.float32)
    return jax.nn.silu(yf).astype(u_val.dtype)


def _to_chunks(t, d):
    b, lp, _ = t.shape
    return t.reshape(b, lp // CHUNK, CHUNK, GLA_HEADS, d).transpose(0, 1, 3, 2, 4)


def gla_group(q, k, v, g, gate_lr, w_gate2, gate_b, norm_g):
    dt = q.dtype
    bsz, seq_len, _ = q.shape
    f = lambda t: t.astype(jnp.float32)
    log_a = jax.nn.log_sigmoid(f(gate_lr) @ f(w_gate2) + f(gate_b)) / GATE_TAU
    pad = lambda t: jnp.pad(f(t), ((0, 0), (META_PAD, 0), (0, 0)))
    qc = _to_chunks(pad(q), GLA_DK) * (GLA_DK ** -0.5)
    kc = _to_chunks(pad(k), GLA_DK)
    vc = _to_chunks(pad(v), GLA_DV)
    bc = jnp.cumsum(_to_chunks(pad(log_a), GLA_DK), axis=3)
    b_last = bc[:, :, :, -1:, :]

    q_in = qc * jnp.exp(bc)
    k_in = kc * jnp.exp(-bc)
    causal = jnp.tril(jnp.ones((CHUNK, CHUNK), dtype=bool))
    scores = jnp.einsum("bnhid,bnhjd->bnhij", q_in, k_in)
    scores = jnp.where(causal, scores, 0.0)
    o_intra = jnp.einsum("bnhij,bnhje->bnhie", scores, vc)

    kv = jnp.einsum("bnhjd,bnhje->bnhde", kc * jnp.exp(b_last - bc), vc)
    decay = jnp.exp(b_last[:, :, :, 0, :])

    def step(state, inp):
        dec, kv_n = inp
        return dec[..., None] * state + kv_n, state

    s0 = jnp.zeros((bsz, GLA_HEADS, GLA_DK, GLA_DV), jnp.float32)
    _, s_prev = lax.scan(step, s0, (decay.swapaxes(0, 1), kv.swapaxes(0, 1)))
    s_prev = s_prev.swapaxes(0, 1)
    o_inter = jnp.einsum("bnhid,bnhde->bnhie", q_in, s_prev)

    o = (o_intra + o_inter).transpose(0, 1, 3, 2, 4).reshape(bsz, -1, GLA_HEADS, GLA_DV)[:, META_PAD:]
    o = o * lax.rsqrt(jnp.mean(o * o, axis=-1, keepdims=True) + RMS_EPS) * f(norm_g)
    o = o * jax.nn.silu(f(g)).reshape(bsz, seq_len, GLA_HEADS, GLA_DV)
    return o.reshape(bsz, seq_len, GLA_V).astype(dt)


def swiglu(x, w_gate, w_up, w_down):
    return (jax.nn.silu(x @ w_gate) * (x @ w_up)) @ w_down


def setup_inputs(seed: int = 0) -> dict:
    key = jax.random.key(seed)
    ks = jax.random.split(key, 20)
    n = lambda k, shape, s: jax.random.normal(k, shape, jnp.float32) * s
    return {
        "x": n(ks[0], (BATCH, SEQ, D_MODEL), 1.0),
        "meta_tokens": n(ks[1], (N_META, D_MODEL), 1.0),
        "norm_mix_g": 1.0 + n(ks[2], (DEPTH, D_MODEL), 0.02),
        "w_in": n(ks[3], (DEPTH, D_MODEL, D_IN), D_MODEL ** -0.5),
        "conv_w": n(ks[4], (DEPTH, CONV_WIDTH, C_CONV), CONV_WIDTH ** -0.5),
        "conv_b": n(ks[5], (DEPTH, C_CONV), 0.02),
        "conv_ln_g": 1.0 + n(ks[6], (DEPTH, C_CONV), 0.02),
        "conv_ln_b": n(ks[7], (DEPTH, C_CONV), 0.02),
        "gla_w_gate2": n(ks[8], (DEPTH, GATE_RANK, GLA_K), GATE_RANK ** -0.5),
        "gla_gate_b": n(ks[9], (DEPTH, GLA_K), 0.1),
        "gla_norm_g": 1.0 + n(ks[10], (DEPTH, GLA_DV), 0.02),
        "w_out": n(ks[11], (DEPTH, D_MIX, D_MODEL), D_MIX ** -0.5),
        "norm_ffn_g": 1.0 + n(ks[12], (DEPTH, D_MODEL), 0.02),
        "w_ffn_gate": n(ks[13], (DEPTH, D_MODEL, D_FF), D_MODEL ** -0.5),
        "w_ffn_up": n(ks[14], (DEPTH, D_MODEL, D_FF), D_MODEL ** -0.5),
        "w_ffn_down": n(ks[15], (DEPTH, D_FF, D_MODEL), D_FF ** -0.5),
        "norm_final_g": 1.0 + n(ks[16], (D_MODEL,), 0.02),
    }


def reference(x, meta_tokens, norm_mix_g, w_in, conv_w, conv_b, conv_ln_g, conv_ln_b,
              gla_w_gate2, gla_gate_b, gla_norm_g, w_out, norm_ffn_g, w_ffn_gate,
              w_ffn_up, w_ffn_down, norm_final_g):
    bsz = x.shape[0]
    meta = jnp.broadcast_to(meta_tokens[None].astype(x.dtype), (bsz, N_META, D_MODEL))
    h = jnp.concatenate([meta, x], axis=1)
    split_idx = list(np.cumsum(IN_SPLITS)[:-1])
    for l in range(DEPTH):
        u = rms_norm(h, norm_mix_g[l]) @ w_in[l]
        c_val, c_gate, q, k, v, g, gate_lr = jnp.split(u, split_idx, axis=-1)
        y_conv = conv_group(c_val, c_gate, conv_w[l], conv_b[l], conv_ln_g[l], conv_ln_b[l])
        y_gla = gla_group(q, k, v, g, gate_lr, gla_w_gate2[l], gla_gate_b[l], gla_norm_g[l])
        h = h + jnp.concatenate([y_conv, y_gla], axis=-1) @ w_out[l]
        h = h + swiglu(rms_norm(h, norm_ffn_g[l]), w_ffn_gate[l], w_ffn_up[l], w_ffn_down[l])
    y = rms_norm(h, norm_final_g)
    return y[:, N_META:]
```

```python
import contextlib
import os
import numpy as np
import concourse.bass as bass
import concourse.mybir as mybir
from concourse.bass_utils import run_bass_kernel_spmd

F32 = mybir.dt.float32
BF16 = mybir.dt.bfloat16
AF = mybir.ActivationFunctionType
ALU = mybir.AluOpType

D = 1024
KC = 8
DIN = 2576
CC = 512
DFF = 2816
NJ = 22
T = 512
NT = 4
NTAP = 31
N_CORES = 8
RMS_EPS = 1e-6
LN_EPS = 1e-5


class Buf:
    def __init__(self, name, psum=False):
        self.name = name
        self.lw = None
        self.rd = []
        self.psum = psum


class Op:
    __slots__ = ("eng", "fn", "deps", "sig", "ticket", "dma", "dkey", "dval", "gi")

    def __init__(self, eng, fn, dma, dkey):
        self.eng = eng
        self.fn = fn
        self.deps = []
        self.sig = False
        self.ticket = 0
        self.dma = dma
        self.dkey = dkey
        self.dval = 0
        self.gi = 0


class Prog:
    ENGS = ("pe", "act", "dve", "pool", "sp")

    def __init__(self):
        self.ops = {e: [] for e in self.ENGS}
        self.n = 0
        self.dcount = {}

    def add(self, eng, fn, R=(), W=(), dma=False, dkey=None):
        if dma and dkey is None:
            dkey = W[0].name if W else R[0].name
        op = Op(eng, fn, dma, dkey)
        op.gi = self.n
        self.n += 1
        deps = {}
        for b in R:
            if b.lw is not None:
                deps[id(b.lw)] = b.lw
            if b.psum:
                for r in b.rd:
                    if r.eng != eng:
                        deps[id(r)] = r
        for b in W:
            if b.lw is not None:
                deps[id(b.lw)] = b.lw
            for r in b.rd:
                deps[id(r)] = r
        deps.pop(id(op), None)
        op.deps = list(deps.values())
        for b in R:
            b.rd.append(op)
        for b in W:
            b.lw = op
            b.rd = []
        if dma:
            self.dcount[dkey] = self.dcount.get(dkey, 0) + 16
            op.dval = self.dcount[dkey]
        self.ops[eng].append(op)
        return op

    @staticmethod
    def _needs_wait(a, b):
        if a.dma:
            return True
        if a.eng == "pe" and b.eng == "pe":
            return False
        return True

    def finalize(self):
        for e in self.ENGS:
            for b in self.ops[e]:
                for a in b.deps:
                    if not a.dma and self._needs_wait(a, b):
                        a.sig = True
        for e in self.ENGS:
            c = 0
            for a in self.ops[e]:
                if a.sig and not a.dma:
                    c += 1
                    a.ticket = c

    def emit(self, eng_name, eng, esem, dsem):
        seen = {}
        for b in self.ops[eng_name]:
            waits = {}
            for a in b.deps:
                if not self._needs_wait(a, b):
                    continue
                if a.dma:
                    s, v = dsem[a.dkey], (self.dcount[a.dkey] if a.dkey == "const" else a.dval)
                else:
                    s, v = esem[a.eng], a.ticket
                k = id(s)
                if k not in waits or waits[k][1] < v:
                    waits[k] = (s, v)
            for k, (s, v) in waits.items():
                if seen.get(k, 0) >= v:
                    continue
                seen[k] = v
                eng.wait_ge(s, v)
            if b.fn is None:
                continue
            ins = b.fn(eng)
            if b.dma:
                ins.then_inc(dsem[b.dkey], 16)
            elif b.sig:
                ins.then_inc(esem[eng_name], 1)


def build_program(nseq, seqlen):
    assert seqlen % T == 0
    nblk_seq = seqlen // T
    ntok = nseq * seqlen
    nc = bass.Bass("TRN2", target_bir_lowering=False)
    P = Prog()

    def din(name, shape, dt=F32):
        return nc.dram_tensor(name, list(shape), dt, kind="ExternalInput").ap()

    xin = din("xin", [T + ntok, D])
    w_in = din("w_in", [D, DIN])
    w_out = din("w_out", [D, D])
    w_fg = din("w_ffn_gate", [D, DFF])
    w_fu = din("w_ffn_up", [D, DFF])
    w_fd = din("w_ffn_down", [DFF, D])
    g_mix_d = din("norm_mix_g", [1, D])
    g_ffn_d = din("norm_ffn_g", [1, D])
    g_fin_d = din("norm_final_g", [1, D])
    convwT_d = din("conv_wT", [128, 4, NTAP])
    cvec_d = din("conv_vecs", [128, 3, 4])
    wg2_d = din("gla_w_gate2", [16, 256])
    gb_d = din("gla_gate_b", [1, 256])
    gng_d = din("gla_norm_g", [128, 1])
    cmat_d = din("c_mats", [128, 4, 128])
    out = nc.dram_tensor("out", [ntok, D], F32, kind="ExternalOutput").ap()

    es = contextlib.ExitStack()
    with es:
        def sb(name, shape, dt=F32):
            return es.enter_context(nc.sbuf_tensor(name, list(shape), dt))

        h = [[sb(f"h{s_}_{t}", [128, D]) for t in range(NT)] for s_ in range(2)]
        st = sb("st", [128, NT, 8])
        NXN = 4
        xn_tm = [sb(f"xn_tm{i}", [128, D], BF16) for i in range(NXN)]
        xT = sb("xT", [128, KC, T], BF16)
        yT = sb("yT", [128, KC, T], BF16)
        vT = sb("vT", [128, 4, 32 + T], BF16)
        hist0 = sb("hist0", [128, 4, 32], BF16)
        NTMP = 5
        tmp = [sb(f"tmp{i}", [128, T]) for i in range(NTMP)]
        glT = sb("glT", [32, T])
        exprev = sb("exprev", [128, NT, 256])
        Eb = sb("Eb", [128, 2, T])
        Einv = sb("Einv", [128, 2, T])
        q_in = sb("q_in", [128, 2, T], BF16)
        k_in = sb("k_in", [128, 2, T], BF16)
        v_tm = sb("v_tm", [128, NT, 512], BF16)
        k_out = sb("k_out", [128, NT, 512], BF16)
        sgT = sb("sgT", [128, 4, T])
        sc_bf = [sb(f"sc_bf{i}", [128, 4, 128], BF16) for i in range(2)]
        S = sb("S", [128, 2, 128])
        S0 = sb("S0", [128, 2, 128])
        S_snap = sb("S_snap", [128, 2 * NT, 2, 128], BF16)
        actT = sb("actT", [128, NJ, T], BF16)

        def act_f32(j0, nch):
            return actT[:, j0:j0 + nch, :].rearrange("p a n -> p (a n)").bitcast(F32)
        ycv = act_f32(14, 8).rearrange("p (c n) -> p c n", c=4)
        ysq = [act_f32(10 + 2 * i, 2) for i in range(2)]
        mu = act_f32(8, 2)
        rsd = act_f32(6, 2)
        NW = 6
        wring = [sb(f"wring{i}", [128, 2048], BF16) for i in range(NW)]
        gmix = sb("gmix", [128, D])
        gffn = sb("gffn", [128, D])
        gfin = sb("gfin", [128, D])
        diag = sb("diag", [128, NTAP, 4, 128], BF16)
        identf = sb("identf", [128, 128])
        ident = sb("ident", [128, 128], BF16)
        ones = sb("ones", [128, 128])
        ones_bf = sb("ones_bf", [128, 128], BF16)
        urev = sb("urev", [128, 128])
        ucum = sb("ucum", [128, 128])
        mask = sb("mask", [128, 128])
        wg2 = sb("wg2", [32, 256])
        wgl = sb("wgl", [128, KC, 16], BF16)
        convwT = sb("convwT", [128, 4, NTAP])
        cvec = sb("cvec", [128, 3, 4])
        gng = sb("gng", [128, 1])

        NPS = 8
        ps = [es.enter_context(nc.psum_tensor(f"ps{i}", [128, 512], F32)) for i in range(NPS)]

        Bh = [[Buf(f"h{s_}_{t}") for t in range(NT)] for s_ in range(2)]
        Bst = [Buf(f"st{t}") for t in range(NT)]
        Bst2 = [Buf(f"stf{t}") for t in range(NT)]
        Bxn = [Buf(f"xn{i}") for i in range(NXN)]
        BxT = [Buf(f"xT{t}") for t in range(NT)]
        ByTc = [Buf(f"yTc{c}") for c in range(4)]
        ByTg = [Buf(f"yTg{t}") for t in range(NT)]
        BvT = [Buf(f"vT{c}") for c in range(4)]
        Bhist = Buf("hist")
        Bhist0 = Buf("hist0")
        Btmp = [Buf(f"tmp{i}") for i in range(NTMP)]
        BglT = Buf("glT")
        Bexprev = [Buf(f"exprev{t}") for t in range(NT)]
        BE = [Buf(f"E{t}") for t in range(NT)]
        BEi = [Buf(f"Ei{t}") for t in range(NT)]
        Bq = [Buf(f"q{hp}") for hp in range(2)]
        Bk = [Buf(f"k{hp}") for hp in range(2)]
        Bvtm = [Buf(f"vtm{t}") for t in range(NT)]
        Bkout = [Buf(f"kout{t}") for t in range(NT)]
        BsgT = [Buf(f"sgT{hh}") for hh in range(4)]
        Bsc = [Buf(f"sc{i}") for i in range(2)]
        BS = Buf("S")
        BS0 = Buf("S0")
        Bsnap = [Buf(f"snap{m}") for m in range(2 * NT)]
        Bdiag = Buf("diag")
        Bact = [Buf(f"act{j}") for j in range(NJ)]
        Bycv = [[Bact[14 + 2 * c], Bact[15 + 2 * c]] for c in range(4)]
        Bysq = [[Bact[10 + 2 * i], Bact[11 + 2 * i]] for i in range(2)]
        Bmu = [Bact[8], Bact[9]]
        Brsd = [Bact[6], Bact[7]]
        Bw = [Buf(f"w{i}") for i in range(NW)]
        Bps = [Buf(f"ps{i}", psum=True) for i in range(NPS)]
        Bconst = Buf("const")

        rr = {"ps": 0, "tmp": 0, "w": 0, "xn": 0, "ysq": 0, "sc": 0}

        def nxt(kind, n):
            i = rr[kind]
            rr[kind] = (i + 1) % n
            return i

        def psum():
            i = nxt("ps", NPS)
            return ps[i], Bps[i]

        def gtmp():
            i = nxt("tmp", NTMP)
            return tmp[i], Btmp[i]

        def wslot():
            i = nxt("w", NW)
            return wring[i], Bw[i]

        pre_row = (NT - 1) * 128
        P.add("sp", lambda e: e.dma_start(out=h[0][NT - 1][:], in_=xin[pre_row:pre_row + 128, :]),
              W=[Bh[0][NT - 1]], dma=True)

        def early_wl(c0):
            s_, B_ = wslot()
            v = s_[:, 0:KC * 256].rearrange("p (k n) -> p k n", k=KC)
            P.add("pool", lambda e: e.dma_start(out=v, in_=w_in[:, c0:c0 + 256].rearrange("(kc p) n -> p kc n", p=128)),
                  W=[B_], dma=True)
            return v, B_
        pre_w = dict(wgv=([early_wl(512), early_wl(768)], [early_wl(0), early_wl(256)]),
                     wk=early_wl(1280), wvt0=early_wl(1536))

        setup_bufs = []

        def sbuf_():
            b = Buf(f"setup{len(setup_bufs)}")
            setup_bufs.append(b)
            return b

        def cload(dst_ap, src_ap, eng="sp", after=(), dkey=None):
            b = sbuf_()
            P.add(eng, lambda e: e.dma_start(out=dst_ap, in_=src_ap), R=list(after), W=[b], dma=True,
                  dkey=dkey or ("const" if eng == "sp" else "constp"))
            return b

        Bid = cload(identf[:], cmat_d[:, 0, :], dkey="c_ident")
        Bgain = {id(gmix): cload(gmix[:], g_mix_d.partition_broadcast(128), dkey="c_gmix")}
        Bident = sbuf_()
        P.add("dve", lambda e: e.tensor_copy(out=ident[:], in_=identf[:]), R=[Bid], W=[Bident])
        Bwg2m = sbuf_()
        P.add("dve", lambda e: e.memset(wg2[:], 0.0), W=[Bwg2m])
        cload(wgl[:], w_in[:, 2560:2576].rearrange("(kc p) n -> p kc n", p=128), eng="pool")
        BconvwT = cload(convwT[:], convwT_d)
        cload(cvec[:], cvec_d)
        cload(gng[:], gng_d)
        cload(wg2[0:16, :], wg2_d, after=[Bwg2m])
        cload(wg2[16:17, :], gb_d, after=[Bwg2m])
        cload(ucum[:], cmat_d[:, 1, :])
        cload(urev[:], cmat_d[:, 2, :])
        cload(mask[:], cmat_d[:, 3, :])
        Bgain[id(gffn)] = cload(gffn[:], g_ffn_d.partition_broadcast(128))
        Bgain[id(gfin)] = cload(gfin[:], g_fin_d.partition_broadcast(128))
        P.add("dve", lambda e: e.memset(ones[:], 1.0), W=[sbuf_()])
        P.add("dve", lambda e: e.memset(ones_bf[:], 1.0), W=[sbuf_()])
        P.add("dve", lambda e: e.memset(glT[:], 1.0), W=[BglT])
        P.add("dve", lambda e: e.memset(k_out[:], 0.0), W=Bkout)
        P.add("dve", lambda e: e.memset(S[:], 0.0), W=[BS])
        P.add("dve", lambda e: e.memset(vT[:, :, 0:32], 0.0), W=[Bhist])
        P.add("dve", lambda e: e.memset(st[:, 0, 7:8], 0.0), R=list(setup_bufs), W=[Bconst])

        def norm_part(hs, t, gb):
            i = nxt("xn", NXN)
            ht, Bht = h[hs][t], Bh[hs][t]
            P.add("act", lambda e: e.activation(out=xn_tm[i][:], in_=ht[:], func=AF.Square,
                                                accum_out=st[:, t, 0:1]), R=[Bht], W=[Bxn[i], Bst[t]])
            P.add("act", lambda e: e.activation(out=st[:, t, 1:2], in_=st[:, t, 0:1], func=AF.Ln,
                                                bias=RMS_EPS, scale=1.0 / D), R=[Bst[t]], W=[Bst[t]])
            P.add("act", lambda e: e.activation(out=st[:, t, 2:3], in_=st[:, t, 1:2], func=AF.Exp,
                                                scale=-0.5), R=[Bst[t]], W=[Bst[t]])
            P.add("dve", lambda e: e.scalar_tensor_tensor(out=xn_tm[i][:], in0=ht[:], scalar=st[:, t, 2:3],
                                                          in1=gb[:], op0=ALU.mult, op1=ALU.mult),
                  R=[Bht, Bst[t], Bgain[id(gb)]], W=[Bxn[i]])
            return i

        def tr_part(i, t, dstT, BdstT):
            pt, Bpt = psum()
            ptb = pt[:].bitcast(BF16).rearrange("p (k n) -> p k n", k=KC)

            def tr(e):
                ins = None
                for kc in range(KC):
                    ins = e.transpose(out=ptb[:, kc, :], in_=xn_tm[i][:, kc * 128:(kc + 1) * 128], identity=ident[:])
                return ins
            P.add("pe", tr, R=[Bxn[i], Bident], W=[Bpt])
            P.add("act", lambda e: e.activation(out=dstT[:, :, t * 128:(t + 1) * 128], in_=ptb, func=AF.Copy),
                  R=[Bpt], W=[BdstT])

        def norm_transpose(hs, t, gb, dstT, BdstT):
            tr_part(norm_part(hs, t, gb), t, dstT, BdstT)

        def wload(dst_ap, src_ap, Bslot):
            P.add("pool", lambda e: e.dma_start(out=dst_ap, in_=src_ap), W=[Bslot], dma=True)

        def w_cols(w, c0, ncols):
            return w[:, c0:c0 + ncols].rearrange("(kc p) n -> p kc n", p=128)

        def proj_fm(wv, Bwv, col0, ncols, srcT, BsrcT, tk=slice(0, T)):
            pt, Bpt = psum()

            def mm(e):
                ins = None
                for kc in range(KC):
                    ins = e.matmul(pt[0:ncols, tk], lhsT=wv[:, kc, col0:col0 + ncols], rhs=srcT[:, kc, tk],
                                   start=(kc == 0), stop=(kc == KC - 1))
                return ins
            P.add("pe", mm, R=[Bwv] + list(BsrcT), W=[Bpt])
            return pt, Bpt

        def wl(w, c0, ncols=256):
            s_, B_ = wslot()
            v = s_[:, 0:KC * ncols].rearrange("p (k n) -> p k n", k=KC)
            wload(v, w_cols(w, c0, ncols), B_)
            return v, B_

        def front(blk):
            hs, row0 = blk["hs"], blk["row0"]
            tiles = [NT - 1] if blk["pre"] else list(range(NT))
            for t in tiles:
                if blk["pre"]:
                    continue
                P.add("sp", lambda e, t=t: e.dma_start(out=h[hs][t][:], in_=xin[row0 + t * 128: row0 + (t + 1) * 128, :]),
                      W=[Bh[hs][t]], dma=True)
            if blk["first"] and not blk["pre"]:
                P.add("pool", lambda e: e.tensor_copy(out=S[:], in_=S0[:]), R=[BS0], W=[BS])
                P.add("pool", lambda e: e.tensor_copy(out=vT[:, :, 0:32], in_=hist0[:]), R=[Bhist0], W=[Bhist])
            blk["xn"] = {t: norm_part(hs, t, gmix) for t in tiles}

        def front_tr(blk):
            for t in sorted(blk["xn"]):
                tr_part(blk["xn"][t], t, xT, BxT[t])

        def rest(blk, nxt_blk):
            hs, out_row0, preamble = blk["hs"], blk["orow0"], blk["pre"]
            tiles = [NT - 1] if preamble else list(range(NT))
            tk = slice((NT - 1) * 128, T) if preamble else slice(0, T)
            if preamble:
                dbufs = []
                for j in range(NTAP):
                    for c in range(4):
                        b = Buf(f"dg{j}_{c}")
                        dbufs.append(b)
                        P.add("dve", lambda e, j=j, c=c: e.tensor_scalar(
                            out=diag[:, j, c, :], in0=identf[:], scalar1=convwT[:, c, j:j + 1], scalar2=None,
                            op0=ALU.mult), R=[Bconst], W=[b])
                P.add("dve", lambda e: e.memset(st[:, 0, 3:4], 0.0), R=dbufs, W=[Bdiag])

            pt, Bpt = proj_fm(wgl, Bconst, 0, 16, xT, BxT, tk)
            P.add("act", lambda e, pt=pt: e.activation(out=glT[0:16, tk], in_=pt[0:16, tk], func=AF.Copy),
                  R=[Bpt], W=[BglT])
            if "wgv" in blk:
                wg, wv = blk["wgv"]
            else:
                wg = [wl(w_in, 512 + 256 * i) for i in range(2)]
                wv = [wl(w_in, 256 * i) for i in range(2)]
            def b2_chunk(c):
                pg, Bpg = proj_fm(wg[c // 2][0], wg[c // 2][1], (c % 2) * 128, 128, xT, BxT, tk)
                tsg, Btsg = gtmp()
                P.add("act", lambda e: e.activation(out=tsg[:, tk], in_=pg[:, tk], func=AF.Sigmoid), R=[Bpg], W=[Btsg])
                pv, Bpv = proj_fm(wv[c // 2][0], wv[c // 2][1], (c % 2) * 128, 128, xT, BxT, tk)
                P.add("dve", lambda e: e.tensor_tensor(out=vT[:, c, 32 + tk.start:32 + tk.stop], in0=pv[:, tk],
                                                       in1=tsg[:, tk], op=ALU.mult),
                      R=[Bpv, Btsg], W=[BvT[c]])

            zst = {}

            def z_tile(t):
                tsl = slice(t * 128, (t + 1) * 128)
                pz, Bpz = psum()
                P.add("pe", lambda e: e.matmul(pz[:, 0:256], lhsT=glT[0:32, tsl], rhs=wg2[:, :], start=True, stop=True),
                      R=[BglT, Bconst], W=[Bpz])
                te, Bte = gtmp()
                P.add("act", lambda e: e.activation(out=te[:, 0:256], in_=pz[:, 0:256], func=AF.Exp, scale=-1.0),
                      R=[Bpz], W=[Bte])
                tsp, Btsp = gtmp()
                P.add("act", lambda e: e.activation(out=tsp[:, 0:256], in_=te[:, 0:256], func=AF.Ln, bias=1.0),
                      R=[Bte], W=[Btsp])
                zst[t] = (tsp, Btsp)

            def decay_tile(t):
                tsl = slice(t * 128, (t + 1) * 128)
                tsp, Btsp = zst[t]
                pr, Bpr = psum()
                P.add("pe", lambda e: e.matmul(pr[:, 0:256], lhsT=urev[:], rhs=tsp[:, 0:256], start=True, stop=True),
                      R=[Btsp, Bconst], W=[Bpr])
                P.add("act", lambda e: e.activation(out=exprev[:, t, :], in_=pr[:, 0:256], func=AF.Exp),
                      R=[Bpr], W=[Bexprev[t]])
                pc, Bpc = psum()

                def mmc(e):
                    ins = None
                    for hp in range(2):
                        ins = e.matmul(pc[:, hp * 128:(hp + 1) * 128], lhsT=tsp[:, hp * 128:(hp + 1) * 128],
                                       rhs=ucum[:], start=True, stop=True)
                    return ins
                P.add("pe", mmc, R=[Btsp, Bconst], W=[Bpc])
                P.add("act", lambda e: e.activation(
                    out=Eb[:, :, tsl], in_=pc[:, 0:256].rearrange("p (a n) -> p a n", a=2), func=AF.Exp),
                    R=[Bpc], W=[BE[t]])
                if not preamble:
                    P.add("act", lambda e: e.activation(
                        out=Einv[:, :, tsl], in_=pc[:, 0:256].rearrange("p (a n) -> p a n", a=2), func=AF.Exp,
                        scale=-1.0), R=[Bpc], W=[BEi[t]])

            if preamble:
                z_tile(NT - 1)
                for c in range(4):
                    b2_chunk(c)
                decay_tile(NT - 1)
            else:
                b2_chunk(0)
                z_tile(0)
                b2_chunk(1)
                z_tile(1)
                decay_tile(0)
                b2_chunk(2)
                z_tile(2)
                decay_tile(1)
                b2_chunk(3)
                z_tile(3)
                decay_tile(2)

            wk, Bwk = blk["wk"] if "wk" in blk else wl(w_in, 1280)
            if not preamble:
                wq, Bwq = wl(w_in, 1024)
                qk = {}
                for hp in range(2):
                    qk[hp] = (proj_fm(wq, Bwq, hp * 128, 128, xT, BxT), proj_fm(wk, Bwk, hp * 128, 128, xT, BxT))
                    if hp == 0:
                        decay_tile(3)
                    (pq, Bpq), (pk, Bpk) = qk[hp]
                    P.add("dve", lambda e, pq=pq, hp=hp: e.scalar_tensor_tensor(
                        out=q_in[:, hp, :], in0=pq[:], scalar=0.125, in1=Eb[:, hp, :], op0=ALU.mult, op1=ALU.mult),
                        R=[Bpq] + BE, W=[Bq[hp]])
                    P.add("dve", lambda e, pk=pk, hp=hp: e.tensor_tensor(
                        out=k_in[:, hp, :], in0=pk[:], in1=Einv[:, hp, :], op=ALU.mult),
                        R=[Bpk] + BEi, W=[Bk[hp]])

            if not preamble:
                p1, Bp1 = psum()
                p2, Bp2 = psum()
                cst = {}

                def conv_ops(c):
                    pcv, Bpcv = psum()

                    def mmconv(e, pcv=pcv, c=c):
                        ins = None
                        for j in range(NTAP):
                            ins = e.matmul(pcv[:], lhsT=diag[:, j, c, :], rhs=vT[:, c, 2 + j:2 + j + T],
                                           start=(j == 0), stop=(j == NTAP - 1))
                        return ins
                    P.add("pe", mmconv, R=[BvT[c], Bhist, Bdiag], W=[Bpcv])
                    P.add("act", lambda e: e.activation(out=ycv[:, c, :], in_=pcv[:], func=AF.Identity,
                                                        bias=cvec[:, 0, c:c + 1]),
                          R=[Bpcv, Bconst], W=Bycv[c])
                    iq = nxt("ysq", 2)
                    P.add("act", lambda e: e.activation(out=ysq[iq][:].bitcast(BF16)[:, 0:T], in_=pcv[:], func=AF.Square,
                                                        bias=cvec[:, 0, c:c + 1]),
                          R=[Bpcv, Bconst], W=Bysq[iq])
                    cst[c] = iq

                def ones_ops(c):
                    iq = cst[c]
                    P.add("pe", lambda e: e.matmul(p1[:], lhsT=ones[:], rhs=ycv[:, c, :], start=(c == 0), stop=(c == 3)),
                          R=Bycv[c] + [Bconst], W=[Bp1])
                    P.add("pe", lambda e: e.matmul(p2[:], lhsT=ones_bf[:], rhs=ysq[iq][:].bitcast(BF16)[:, 0:T],
                                                   start=(c == 0), stop=(c == 3)),
                          R=Bysq[iq] + [Bconst], W=[Bp2])

                conv_ops(0)
                conv_ops(1)
                ones_ops(0)
                conv_ops(2)
                ones_ops(1)
                conv_ops(3)
                ones_ops(2)
                ones_ops(3)
                P.add("dve", lambda e: e.tensor_scalar(out=mu[:], in0=p1[:], scalar1=1.0 / CC, scalar2=None, op0=ALU.mult),
                      R=[Bp1], W=Bmu)
                P.add("act", lambda e: e.activation(out=rsd[:], in_=p1[:], func=AF.Square, scale=1.0 / CC),
                      R=[Bp1], W=Brsd)
                P.add("dve", lambda e: e.scalar_tensor_tensor(out=rsd[:], in0=p2[:], scalar=1.0 / CC, in1=rsd[:],
                                                              op0=ALU.mult, op1=ALU.subtract), R=[Bp2] + Brsd, W=Brsd)

                def conv_normalize():
                    P.add("act", lambda e: e.activation(out=rsd[:], in_=rsd[:], func=AF.Ln, bias=LN_EPS), R=Brsd, W=Brsd)
                    P.add("act", lambda e: e.activation(out=rsd[:], in_=rsd[:], func=AF.Exp, scale=-0.5), R=Brsd, W=Brsd)
                    for c in range(4):
                        t1, Bt1 = gtmp()
                        P.add("dve", lambda e, t1=t1, c=c: e.tensor_tensor(out=t1[:], in0=ycv[:, c, :], in1=mu[:],
                                                                           op=ALU.subtract), R=Bycv[c] + Bmu, W=[Bt1])
                        P.add("dve", lambda e, t1=t1: e.tensor_tensor(out=t1[:], in0=t1[:], in1=rsd[:], op=ALU.mult),
                              R=[Bt1] + Brsd, W=[Bt1])
                        P.add("act", lambda e, t1=t1, c=c: e.activation(out=yT[:, c, :], in_=t1[:], func=AF.Silu,
                                                                        bias=cvec[:, 2, c:c + 1], scale=cvec[:, 1, c:c + 1]),
                              R=[Bt1, Bconst], W=[ByTc[c]])
                conv_normalize()

            wvt = [blk["wvt0"] if "wvt0" in blk else wl(w_in, 1536), wl(w_in, 1792)]
            for t in tiles:
                tsl = slice(t * 128, (t + 1) * 128)
                pv, Bpv = psum()

                def mmv(e, pv=pv, tsl=tsl):
                    ins = None
                    for hf in range(2):
                        for kc in range(KC):
                            ins = e.matmul(pv[:, hf * 256:(hf + 1) * 256], lhsT=xT[:, kc, tsl], rhs=wvt[hf][0][:, kc, :],
                                           start=(kc == 0), stop=(kc == KC - 1))
                    return ins
                P.add("pe", mmv, R=[wvt[0][1], wvt[1][1], BxT[t]], W=[Bpv])
                P.add("act", lambda e, pv=pv, t=t: e.activation(out=v_tm[:, t, :], in_=pv[:], func=AF.Copy),
                      R=[Bpv], W=[Bvtm[t]])
                pk, Bpk = psum()

                def mmk(e, pk=pk, tsl=tsl):
                    ins = None
                    for kc in range(KC):
                        ins = e.matmul(pk[:, 0:256], lhsT=xT[:, kc, tsl], rhs=wk[:, kc, :], start=(kc == 0),
                                       stop=(kc == KC - 1))
                    return ins
                P.add("pe", mmk, R=[Bwk, BxT[t]], W=[Bpk])
                ko = k_out[:, t, :].rearrange("p (a b n) -> p a b n", a=2, b=2)
                pk4 = pk[:, 0:256].rearrange("p (a b n) -> p a b n", a=2, b=2)
                er4 = exprev[:, t, :].rearrange("p (a b n) -> p a b n", a=2, b=2)
                for hh in range(2):
                    P.add("dve", lambda e, ko=ko, pk4=pk4, er4=er4, hh=hh: e.tensor_tensor(
                        out=ko[:, :, hh, hh * 64:hh * 64 + 64], in0=pk4[:, :, hh, :], in1=er4[:, :, hh, :],
                        op=ALU.mult), R=[Bpk, Bexprev[t]], W=[Bkout[t]])

            if not preamble:
                wgg = [wl(w_in, 2048 + 256 * i) for i in range(2)]
                for hh in range(4):
                    pg, Bpg = proj_fm(wgg[hh // 2][0], wgg[hh // 2][1], (hh % 2) * 128, 128, xT, BxT)
                    P.add("act", lambda e, pg=pg, hh=hh: e.activation(out=sgT[:, hh, :], in_=pg[:], func=AF.Silu),
                          R=[Bpg], W=[BsgT[hh]])

            scs = {}
            def d2_scores(t):
                tsl = slice(t * 128, (t + 1) * 128)
                pscs = [psum(), psum()]

                def mmsc(e, pscs=pscs, tsl=tsl):
                    ins = None
                    for hh in range(2):
                        for hp in range(2):
                            ins = e.matmul(pscs[hh][0][:, hp * 128:(hp + 1) * 128],
                                           lhsT=k_in[hh * 64:(hh + 1) * 64, hp, tsl],
                                           rhs=q_in[hh * 64:(hh + 1) * 64, hp, tsl], start=True, stop=True)
                    return ins
                P.add("pe", mmsc, R=Bq + Bk, W=[pscs[0][1], pscs[1][1]])
                isc = nxt("sc", 2)
                scv = sc_bf[isc][:].rearrange("p (a b) n -> p a b n", b=2)
                for hh in range(2):
                    P.add("dve", lambda e, pscs=pscs, scv=scv, hh=hh: e.tensor_tensor(
                        out=scv[:, :, hh, :], in0=pscs[hh][0][:, 0:256].rearrange("p (a n) -> p a n", a=2),
                        in1=mask[:].unsqueeze(1).to_broadcast([128, 2, 128]), op=ALU.mult),
                        R=[pscs[hh][1], Bconst], W=[Bsc[isc]])
                return isc

            if not preamble:
                scs[0] = d2_scores(0)
                scs[1] = d2_scores(1)
            kvb = [psum() for _ in range(4)]

            def kv_ap(t, n, hp):
                return kvb[n * 2 + t // 2][0][:, (t % 2) * 256 + hp * 128:(t % 2) * 256 + (hp + 1) * 128]

            for n in range(2):
                def mmkv(e, n=n):
                    ins = None
                    rows = slice(n * 64, (n + 1) * 64)
                    for t in tiles:
                        for hp in range(2):
                            for hh in range(2):
                                hd = 2 * hp + hh
                                ins = e.matmul(kv_ap(t, n, hp), lhsT=k_out[rows, t, hd * 128:(hd + 1) * 128],
                                               rhs=v_tm[rows, t, hd * 128:(hd + 1) * 128], start=(hh == 0), stop=(hh == 1))
                    return ins
                P.add("pe", mmkv, R=Bkout + Bvtm, W=[kvb[n * 2][1], kvb[n * 2 + 1][1]])
            for m in range(2 * tiles[0], 2 * NT):
                t, n = divmod(m, 2)
                if not preamble:
                    P.add("dve", lambda e, m=m: e.tensor_copy(out=S_snap[:, m, :, :], in_=S[:]),
                          R=[BS], W=[Bsnap[m]])
                lastc = t * 128 + n * 64 + 63
                for hp in range(2):
                    P.add("dve", lambda e, t=t, n=n, hp=hp, lastc=lastc: e.scalar_tensor_tensor(
                        out=S[:, hp, :], in0=S[:, hp, :], scalar=Eb[:, hp, lastc:lastc + 1],
                        in1=kv_ap(t, n, hp), op0=ALU.mult, op1=ALU.add),
                        R=[BS, BE[t], kvb[n * 2 + t // 2][1]], W=[BS])

            if not preamble:
                wo = [wl(w_out, 256 * i) for i in range(4)]

                def wout_tile(t):
                    tsl = slice(t * 128, (t + 1) * 128)
                    for cb in range(2):
                        pt, Bpt = psum()

                        def mmo2(e, pt=pt, tsl=tsl, cb=cb):
                            ins = None
                            for hf in range(2):
                                for kc in range(KC):
                                    ins = e.matmul(pt[:, hf * 256:(hf + 1) * 256], lhsT=yT[:, kc, tsl],
                                                   rhs=wo[2 * cb + hf][0][:, kc, :], start=(kc == 0), stop=(kc == KC - 1))
                            return ins
                        P.add("pe", mmo2, R=[wo[2 * cb][1], wo[2 * cb + 1][1], ByTg[t]] + ByTc, W=[Bpt])
                        P.add("dve", lambda e, pt=pt, t=t, cb=cb: e.tensor_tensor(
                            out=h[hs][t][:, cb * 512:(cb + 1) * 512], in0=h[hs][t][:, cb * 512:(cb + 1) * 512], in1=pt[:],
                            op=ALU.add), R=[Bpt, Bh[hs][t]], W=[Bh[hs][t]])


                def d2_out(t, isc):
                    tsl = slice(t * 128, (t + 1) * 128)
                    po, Bpo = psum()
                    for n in range(2):
                        m = 2 * t + n

                        def mmo(e, po=po, isc=isc, n=n, t=t, m=m):
                            ins = None
                            for hd in range(4):
                                hp, hh = hd // 2, hd % 2
                                oc = slice(hd * 128 + n * 64, hd * 128 + n * 64 + 64)
                                e.matmul(po[:, oc], lhsT=v_tm[:, t, hd * 128:(hd + 1) * 128],
                                         rhs=sc_bf[isc][:, hd, n * 64:(n + 1) * 64], start=True, stop=False)
                                ins = e.matmul(po[:, oc], lhsT=S_snap[hh * 64:(hh + 1) * 64, m, hp, :],
                                               rhs=q_in[hh * 64:(hh + 1) * 64, hp, t * 128 + n * 64: t * 128 + n * 64 + 64],
                                               start=False, stop=True)
                            return ins
                        P.add("pe", mmo, R=[Bvtm[t], Bsc[isc], Bsnap[m]] + Bq, W=[Bpo])
                    return po, Bpo

                def d2_norm(t, po, Bpo):
                    tsl = slice(t * 128, (t + 1) * 128)
                    tq, Btq = gtmp()
                    P.add("act", lambda e, po=po, tq=tq: e.activation(out=tq[:].bitcast(BF16)[:, 0:T], in_=po[:], func=AF.Square),
                          R=[Bpo], W=[Btq])
                    pss, Bpss = psum()
                    P.add("pe", lambda e, pss=pss, tq=tq: e.matmul(pss[:], lhsT=ones_bf[:], rhs=tq[:].bitcast(BF16)[:, 0:T],
                                                                   start=True, stop=True),
                          R=[Btq, Bconst], W=[Bpss])
                    tr_, Btr = gtmp()
                    P.add("act", lambda e, pss=pss, tr_=tr_: e.activation(out=tr_[:], in_=pss[:], func=AF.Ln,
                                                                          bias=RMS_EPS, scale=1.0 / 128),
                          R=[Bpss], W=[Btr])
                    P.add("act", lambda e, tr_=tr_: e.activation(out=tr_[:], in_=tr_[:], func=AF.Exp, scale=-0.5),
                          R=[Btr], W=[Btr])
                    P.add("dve", lambda e, po=po, tr_=tr_: e.tensor_tensor(out=tr_[:], in0=po[:], in1=tr_[:], op=ALU.mult),
                          R=[Bpo, Btr], W=[Btr])
                    P.add("dve", lambda e, tr_=tr_, tsl=tsl: e.scalar_tensor_tensor(
                        out=yT[:, 4:8, tsl], in0=tr_[:].rearrange("p (a n) -> p a n", a=4), scalar=gng[:, 0:1],
                        in1=sgT[:, :, tsl], op0=ALU.mult, op1=ALU.mult),
                        R=[Btr, Bconst] + BsgT, W=[ByTg[t]])


                def nt(t):
                    norm_transpose(hs, t, gffn, xT, BxT[t])

                o0 = d2_out(0, scs[0])
                scs[2] = d2_scores(2)
                o1 = d2_out(1, scs[1])
                d2_norm(0, *o0)
                scs[3] = d2_scores(3)
                o2 = d2_out(2, scs[2])
                d2_norm(1, *o1)
                wout_tile(0)
                o3 = d2_out(3, scs[3])
                d2_norm(2, *o2)
                wout_tile(1)
                nt(0)
                d2_norm(3, *o3)
                wout_tile(2)
                nt(1)
                wout_tile(3)
                nt(2)
                nt(3)

            P.add("pool", lambda e: e.tensor_copy(out=vT[:, :, 0:32], in_=vT[:, :, T:T + 32]), R=BvT, W=[Bhist])
            if preamble:
                P.add("pool", lambda e: e.tensor_copy(out=S0[:], in_=S[:]), R=[BS], W=[BS0])
                P.add("pool", lambda e: e.tensor_copy(out=hist0[:], in_=vT[:, :, 0:32]), R=[Bhist], W=[Bhist0])
                if nxt_blk is not None:
                    front(nxt_blk)
                    front_tr(nxt_blk)
                return

            if nxt_blk is not None:
                front(nxt_blk)
            for j0 in range(0, NJ, 2):
                wgv, Bwg_ = wl(w_fg, j0 * 128)
                wuv, Bwu_ = wl(w_fu, j0 * 128)
                for jj in range(2):
                    j = j0 + jj
                    pg, Bpg = proj_fm(wgv, Bwg_, jj * 128, 128, xT, BxT)
                    pu, Bpu = proj_fm(wuv, Bwu_, jj * 128, 128, xT, BxT)
                    ts_, Bts = gtmp()
                    P.add("act", lambda e, pg=pg, ts_=ts_: e.activation(out=ts_[:], in_=pg[:], func=AF.Silu),
                          R=[Bpg], W=[Bts])
                    P.add("dve", lambda e, pu=pu, ts_=ts_, j=j: e.tensor_tensor(out=actT[:, j, :], in0=pu[:], in1=ts_[:],
                                                                               op=ALU.mult), R=[Bpu, Bts], W=[Bact[j]])
            if nxt_blk is not None:
                front_tr(nxt_blk)
            for cb in range(2):
                acc = [psum() for _ in range(NT)]
                for j0 in range(0, NJ, 2):
                    wd_, Bwd = wslot()
                    wd = wd_[:, 0:1024].rearrange("p (j n) -> p j n", j=2)
                    wload(wd, w_fd[j0 * 128:(j0 + 2) * 128, cb * 512:(cb + 1) * 512].rearrange("(j p) n -> p j n", p=128), Bwd)

                    def mmd(e, wd=wd, j0=j0, acc=acc):
                        ins = None
                        for jj in range(2):
                            j = j0 + jj
                            for t in range(NT):
                                ins = e.matmul(acc[t][0][:], lhsT=actT[:, j, t * 128:(t + 1) * 128], rhs=wd[:, jj, :],
                                               start=(j == 0), stop=(j == NJ - 1))
                        return ins
                    P.add("pe", mmd, R=[Bwd] + Bact[j0:j0 + 2], W=[a[1] for a in acc])
                    if cb == 1 and j0 == NJ - 2 and nxt_blk is not None:
                        nxt_blk["wgv"] = ([wl(w_in, 512 + 256 * i) for i in range(2)],
                                          [wl(w_in, 256 * i) for i in range(2)])
                for t in range(NT):
                    P.add("dve", lambda e, t=t, cb=cb, a=acc[t][0]: e.tensor_tensor(
                        out=h[hs][t][:, cb * 512:(cb + 1) * 512], in0=h[hs][t][:, cb * 512:(cb + 1) * 512], in1=a[:],
                        op=ALU.add), R=[acc[t][1], Bh[hs][t]], W=[Bh[hs][t]])
            for t in range(NT):
                tj, Btj = gtmp()
                ht, Bht = h[hs][t], Bh[hs][t]
                P.add("act", lambda e, t=t, tj=tj, ht=ht: e.activation(out=tj[:].bitcast(BF16), in_=ht[:], func=AF.Square,
                                                                       accum_out=st[:, t, 4:5]),
                      R=[Bht], W=[Btj, Bst2[t]])
                P.add("act", lambda e, t=t: e.activation(out=st[:, t, 5:6], in_=st[:, t, 4:5], func=AF.Ln,
                                                         bias=RMS_EPS, scale=1.0 / D), R=[Bst2[t]], W=[Bst2[t]])
                P.add("act", lambda e, t=t: e.activation(out=st[:, t, 6:7], in_=st[:, t, 5:6], func=AF.Exp,
                                                         scale=-0.5), R=[Bst2[t]], W=[Bst2[t]])
                P.add("dve", lambda e, t=t, ht=ht: e.scalar_tensor_tensor(out=ht[:], in0=ht[:], scalar=st[:, t, 6:7],
                                                                          in1=gfin[:], op0=ALU.mult, op1=ALU.mult),
                      R=[Bht, Bst2[t], Bconst], W=[Bht])
                P.add("sp", lambda e, t=t, ht=ht: e.dma_start(out=out[out_row0 + t * 128: out_row0 + (t + 1) * 128, :],
                                                              in_=ht[:]),
                      R=[Bht], dma=True, dkey=Bht.name)

        blocks = [dict(hs=0, row0=0, orow0=0, first=False, pre=True, **pre_w)]
        for s_i in range(nseq):
            for b in range(nblk_seq):
                r = s_i * seqlen + b * T
                blocks.append(dict(hs=len(blocks) % 2, row0=T + r, orow0=r, first=(b == 0), pre=False))
        front(blocks[0])
        front_tr(blocks[0])
        for bi, blk in enumerate(blocks):
            rest(blk, blocks[bi + 1] if bi + 1 < len(blocks) else None)
        P.add("sp", None, W=Bh[0] + Bh[1])
        P.finalize()
        dkeys = sorted(P.dcount.keys())
        esem = {e: es.enter_context(nc.semaphore(f"sem_{e}")) for e in Prog.ENGS}
        dsem = {k: es.enter_context(nc.semaphore(f"dsem_{k}")) for k in dkeys}
        with nc.Block() as blk:
            @blk.tensor
            def _(e):
                P.emit("pe", e, esem, dsem)

            @blk.scalar
            def _(e):
                P.emit("act", e, esem, dsem)

            @blk.vector
            def _(e):
                P.emit("dve", e, esem, dsem)

            @blk.gpsimd
            def _(e):
                P.emit("pool", e, esem, dsem)

            @blk.sync
            def _(e):
                P.emit("sp", e, esem, dsem)
    return nc


def make_in_maps(inputs, nseq, seqlen, n_cores):
    x = np.ascontiguousarray(inputs["x"], dtype=np.float32)
    meta = np.asarray(inputs["meta_tokens"], dtype=np.float32)
    pre = np.zeros((T, D), np.float32)
    pre[T - meta.shape[0]:] = meta
    conv_wT = np.ascontiguousarray(
        np.asarray(inputs["conv_w"][0], np.float32).T.reshape(4, 128, NTAP).transpose(1, 0, 2))
    vecs = np.stack([np.asarray(inputs[k][0], np.float32).reshape(4, 128).T
                     for k in ("conv_b", "conv_ln_g", "conv_ln_b")], axis=1)
    common = {
        "w_in": np.ascontiguousarray(inputs["w_in"][0], dtype=np.float32),
        "w_out": np.ascontiguousarray(inputs["w_out"][0], dtype=np.float32),
        "w_ffn_gate": np.ascontiguousarray(inputs["w_ffn_gate"][0], dtype=np.float32),
        "w_ffn_up": np.ascontiguousarray(inputs["w_ffn_up"][0], dtype=np.float32),
        "w_ffn_down": np.ascontiguousarray(inputs["w_ffn_down"][0], dtype=np.float32),
        "norm_mix_g": np.asarray(inputs["norm_mix_g"], np.float32).reshape(1, D),
        "norm_ffn_g": np.asarray(inputs["norm_ffn_g"], np.float32).reshape(1, D),
        "norm_final_g": np.asarray(inputs["norm_final_g"], np.float32).reshape(1, D),
        "conv_wT": conv_wT,
        "conv_vecs": np.ascontiguousarray(vecs),
        "gla_w_gate2": np.ascontiguousarray(inputs["gla_w_gate2"][0], dtype=np.float32),
        "gla_gate_b": np.asarray(inputs["gla_gate_b"], np.float32).reshape(1, 256),
        "gla_norm_g": np.asarray(inputs["gla_norm_g"], np.float32).reshape(128, 1),
    }
    jj, ii = np.meshgrid(np.arange(128), np.arange(128), indexing="ij")
    same = (jj // 64) == (ii // 64)
    cm = np.zeros((128, 4, 128), np.float32)
    cm[:, 0, :] = (jj == ii)
    cm[:, 1, :] = np.where(same & (jj <= ii), -1.0 / 16, 0.0)
    cm[:, 2, :] = np.where(same & (jj > ii), -1.0 / 16, 0.0)
    cm[:, 3, :] = (same & (jj <= ii))
    common["c_mats"] = cm
    maps = []
    for c in range(n_cores):
        xc = x[c * nseq:(c + 1) * nseq].reshape(nseq * seqlen, D)
        m = dict(common)
        m["xin"] = np.concatenate([pre, xc], axis=0)
        maps.append(m)
    return maps


def kernel(**inputs):
    x = inputs["x"]
    bsz, seqlen, _ = x.shape
    nseq = bsz // N_CORES
    nc = build_program(nseq, seqlen)
    in_maps = make_in_maps(inputs, nseq, seqlen, N_CORES)
    res = run_bass_kernel_spmd(nc, in_maps, core_ids=list(range(N_CORES)))
    outs = [np.asarray(r["out"], dtype=np.float32).reshape(nseq, seqlen, D) for r in res.results]
    return np.concatenate(outs, axis=0)
```

```python
import contextlib
import os
import numpy as np
import concourse.bass as bass
import concourse.mybir as mybir
from concourse.bass_utils import run_bass_kernel_spmd

F32 = mybir.dt.float32
BF16 = mybir.dt.bfloat16
AF = mybir.ActivationFunctionType
ALU = mybir.AluOpType

D = 1024
KC = 8
DIN = 2576
CC = 512
DFF = 2816
NJ = 22
T = 512
NT = 4
NTAP = 31
N_CORES = 8
RMS_EPS = 1e-6
LN_EPS = 1e-5


class Buf:
    def __init__(self, name, psum=False):
        self.name = name
        self.lw = None
        self.rd = []
        self.psum = psum


class Op:
    __slots__ = ("eng", "fn", "deps", "sig", "ticket", "dma", "dkey", "dval", "gi")

    def __init__(self, eng, fn, dma, dkey):
        self.eng = eng
        self.fn = fn
        self.deps = []
        self.sig = False
        self.ticket = 0
        self.dma = dma
        self.dkey = dkey
        self.dval = 0
        self.gi = 0


class Prog:
    ENGS = ("pe", "act", "dve", "pool", "sp")

    def __init__(self):
        self.ops = {e: [] for e in self.ENGS}
        self.n = 0
        self.dcount = {}

    def add(self, eng, fn, R=(), W=(), dma=False, dkey=None):
        if dma and dkey is None:
            dkey = W[0].name if W else R[0].name
        op = Op(eng, fn, dma, dkey)
        op.gi = self.n
        self.n += 1
        deps = {}
        for b in R:
            if b.lw is not None:
                deps[id(b.lw)] = b.lw
            if b.psum:
                for r in b.rd:
                    if r.eng != eng:
                        deps[id(r)] = r
        for b in W:
            if b.lw is not None:
                deps[id(b.lw)] = b.lw
            for r in b.rd:
                deps[id(r)] = r
        deps.pop(id(op), None)
        op.deps = list(deps.values())
        for b in R:
            b.rd.append(op)
        for b in W:
            b.lw = op
            b.rd = []
        if dma:
            self.dcount[dkey] = self.dcount.get(dkey, 0) + 16
            op.dval = self.dcount[dkey]
        self.ops[eng].append(op)
        return op

    @staticmethod
    def _needs_wait(a, b):
        if a.dma:
            return True
        if a.eng == "pe" and b.eng == "pe":
            return False
        return True

    def finalize(self):
        for e in self.ENGS:
            for b in self.ops[e]:
                for a in b.deps:
                    if not a.dma and self._needs_wait(a, b):
                        a.sig = True
        for e in self.ENGS:
            c = 0
            for a in self.ops[e]:
                if a.sig and not a.dma:
                    c += 1
                    a.ticket = c

    def emit(self, eng_name, eng, esem, dsem):
        seen = {}
        for b in self.ops[eng_name]:
            waits = {}
            for a in b.deps:
                if not self._needs_wait(a, b):
                    continue
                if a.dma:
                    s, v = dsem[a.dkey], (self.dcount[a.dkey] if a.dkey == "const" else a.dval)
                else:
                    s, v = esem[a.eng], a.ticket
                k = id(s)
                if k not in waits or waits[k][1] < v:
                    waits[k] = (s, v)
            for k, (s, v) in waits.items():
                if seen.get(k, 0) >= v:
                    continue
                seen[k] = v
                eng.wait_ge(s, v)
            if b.fn is None:
                continue
            ins = b.fn(eng)
            if b.dma:
                ins.then_inc(dsem[b.dkey], 16)
            elif b.sig:
                ins.then_inc(esem[eng_name], 1)


def build_program(nseq, seqlen):
    assert seqlen % T == 0
    nblk_seq = seqlen // T
    ntok = nseq * seqlen
    nc = bass.Bass("TRN2", target_bir_lowering=False)
    P = Prog()

    def din(name, shape, dt=F32):
        return nc.dram_tensor(name, list(shape), dt, kind="ExternalInput").ap()

    xin = din("xin", [T + ntok, D])
    w_in = din("w_in", [D, DIN])
    w_out = din("w_out", [D, D])
    w_fg = din("w_ffn_gate", [D, DFF])
    w_fu = din("w_ffn_up", [D, DFF])
    w_fd = din("w_ffn_down", [DFF, D])
    g_mix_d = din("norm_mix_g", [1, D])
    g_ffn_d = din("norm_ffn_g", [1, D])
    g_fin_d = din("norm_final_g", [1, D])
    convwT_d = din("conv_wT", [128, 4, NTAP])
    cvec_d = din("conv_vecs", [128, 3, 4])
    wg2_d = din("gla_w_gate2", [16, 256])
    gb_d = din("gla_gate_b", [1, 256])
    gng_d = din("gla_norm_g", [128, 1])
    cmat_d = din("c_mats", [128, 4, 128])
    out = nc.dram_tensor("out", [ntok, D], F32, kind="ExternalOutput").ap()

    es = contextlib.ExitStack()
    with es:
        def sb(name, shape, dt=F32):
            return es.enter_context(nc.sbuf_tensor(name, list(shape), dt))

        h = [[sb(f"h{s_}_{t}", [128, D]) for t in range(NT)] for s_ in range(2)]
        st = sb("st", [128, NT, 8])
        NXN = 4
        xn_tm = [sb(f"xn_tm{i}", [128, D], BF16) for i in range(NXN)]
        xT = sb("xT", [128, KC, T], BF16)
        yT = sb("yT", [128, KC, T], BF16)
        vT = sb("vT", [128, 4, 32 + T], BF16)
        hist0 = sb("hist0", [128, 4, 32], BF16)
        NTMP = 5
        tmp = [sb(f"tmp{i}", [128, T]) for i in range(NTMP)]
        glT = sb("glT", [32, T])
        exprev = sb("exprev", [128, NT, 256])
        Eb = sb("Eb", [128, 2, T])
        Einv = sb("Einv", [128, 2, T])
        q_in = sb("q_in", [128, 2, T], BF16)
        k_in = sb("k_in", [128, 2, T], BF16)
        v_tm = sb("v_tm", [128, NT, 512], BF16)
        k_out = sb("k_out", [128, NT, 512], BF16)
        sgT = sb("sgT", [128, 4, T])
        sc_bf = [sb(f"sc_bf{i}", [128, 4, 128], BF16) for i in range(2)]
        S = sb("S", [128, 2, 128])
        S0 = sb("S0", [128, 2, 128])
        S_snap = sb("S_snap", [128, 2 * NT, 2, 128], BF16)
        actT = sb("actT", [128, NJ, T], BF16)

        def act_f32(j0, nch):
            return actT[:, j0:j0 + nch, :].rearrange("p a n -> p (a n)").bitcast(F32)
        ycv = act_f32(14, 8).rearrange("p (c n) -> p c n", c=4)
        ysq = [act_f32(10 + 2 * i, 2) for i in range(2)]
        mu = act_f32(8, 2)
        rsd = act_f32(6, 2)
        NW = 6
        wring = [sb(f"wring{i}", [128, 2048], BF16) for i in range(NW)]
        gmix = sb("gmix", [128, D])
        gffn = sb("gffn", [128, D])
        gfin = sb("gfin", [128, D])
        diag = sb("diag", [128, NTAP, 4, 128], BF16)
        identf = sb("identf", [128, 128])
        ident = sb("ident", [128, 128], BF16)
        ones = sb("ones", [128, 128])
        ones_bf = sb("ones_bf", [128, 128], BF16)
        urev = sb("urev", [128, 128])
        ucum = sb("ucum", [128, 128])
        mask = sb("mask", [128, 128])
        wg2 = sb("wg2", [32, 256])
        wgl = sb("wgl", [128, KC, 16], BF16)
        convwT = sb("convwT", [128, 4, NTAP])
        cvec = sb("cvec", [128, 3, 4])
        gng = sb("gng", [128, 1])

        NPS = 8
        ps = [es.enter_context(nc.psum_tensor(f"ps{i}", [128, 512], F32)) for i in range(NPS)]

        Bh = [[Buf(f"h{s_}_{t}") for t in range(NT)] for s_ in range(2)]
        Bst = [Buf(f"st{t}") for t in range(NT)]
        Bst2 = [Buf(f"stf{t}") for t in range(NT)]
        Bxn = [Buf(f"xn{i}") for i in range(NXN)]
        BxT = [Buf(f"xT{t}") for t in range(NT)]
        ByTc = [Buf(f"yTc{c}") for c in range(4)]
        ByTg = [Buf(f"yTg{t}") for t in range(NT)]
        BvT = [Buf(f"vT{c}") for c in range(4)]
        Bhist = Buf("hist")
        Bhist0 = Buf("hist0")
        Btmp = [Buf(f"tmp{i}") for i in range(NTMP)]
        BglT = Buf("glT")
        Bexprev = [Buf(f"exprev{t}") for t in range(NT)]
        BE = [Buf(f"E{t}") for t in range(NT)]
        BEi = [Buf(f"Ei{t}") for t in range(NT)]
        Bq = [Buf(f"q{hp}") for hp in range(2)]
        Bk = [Buf(f"k{hp}") for hp in range(2)]
        Bvtm = [Buf(f"vtm{t}") for t in range(NT)]
        Bkout = [Buf(f"kout{t}") for t in range(NT)]
        BsgT = [Buf(f"sgT{hh}") for hh in range(4)]
        Bsc = [Buf(f"sc{i}") for i in range(2)]
        BS = Buf("S")
        BS0 = Buf("S0")
        Bsnap = [Buf(f"snap{m}") for m in range(2 * NT)]
        Bdiag = Buf("diag")
        Bact = [Buf(f"act{j}") for j in range(NJ)]
        Bycv = [[Bact[14 + 2 * c], Bact[15 + 2 * c]] for c in range(4)]
        Bysq = [[Bact[10 + 2 * i], Bact[11 + 2 * i]] for i in range(2)]
        Bmu = [Bact[8], Bact[9]]
        Brsd = [Bact[6], Bact[7]]
        Bw = [Buf(f"w{i}") for i in range(NW)]
        Bps = [Buf(f"ps{i}", psum=True) for i in range(NPS)]
        Bconst = Buf("const")

        rr = {"ps": 0, "tmp": 0, "w": 0, "xn": 0, "ysq": 0, "sc": 0}

        def nxt(kind, n):
            i = rr[kind]
            rr[kind] = (i + 1) % n
            return i

        def psum():
            i = nxt("ps", NPS)
            return ps[i], Bps[i]

        def gtmp():
            i = nxt("tmp", NTMP)
            return tmp[i], Btmp[i]

        def wslot():
            i = nxt("w", NW)
            return wring[i], Bw[i]

        pre_row = (NT - 1) * 128
        P.add("sp", lambda e: e.dma_start(out=h[0][NT - 1][:], in_=xin[pre_row:pre_row + 128, :]),
              W=[Bh[0][NT - 1]], dma=True)

        def early_wl(c0):
            s_, B_ = wslot()
            v = s_[:, 0:KC * 256].rearrange("p (k n) -> p k n", k=KC)
            P.add("pool", lambda e: e.dma_start(out=v, in_=w_in[:, c0:c0 + 256].rearrange("(kc p) n -> p kc n", p=128)),
                  W=[B_], dma=True)
            return v, B_
        pre_w = dict(wgv=([early_wl(512), early_wl(768)], [early_wl(0), early_wl(256)]),
                     wk=early_wl(1280), wvt0=early_wl(1536))

        setup_bufs = []

        def sbuf_():
            b = Buf(f"setup{len(setup_bufs)}")
            setup_bufs.append(b)
            return b

        def cload(dst_ap, src_ap, eng="sp", after=(), dkey=None):
            b = sbuf_()
            P.add(eng, lambda e: e.dma_start(out=dst_ap, in_=src_ap), R=list(after), W=[b], dma=True,
                  dkey=dkey or ("const" if eng == "sp" else "constp"))
            return b

        Bid = cload(identf[:], cmat_d[:, 0, :], dkey="c_ident")
        Bgain = {id(gmix): cload(gmix[:], g_mix_d.partition_broadcast(128), dkey="c_gmix")}
        Bident = sbuf_()
        P.add("dve", lambda e: e.tensor_copy(out=ident[:], in_=identf[:]), R=[Bid], W=[Bident])
        Bwg2m = sbuf_()
        P.add("dve", lambda e: e.memset(wg2[:], 0.0), W=[Bwg2m])
        cload(wgl[:], w_in[:, 2560:2576].rearrange("(kc p) n -> p kc n", p=128), eng="pool")
        BconvwT = cload(convwT[:], convwT_d)
        cload(cvec[:], cvec_d)
        cload(gng[:], gng_d)
        cload(wg2[0:16, :], wg2_d, after=[Bwg2m])
        cload(wg2[16:17, :], gb_d, after=[Bwg2m])
        cload(ucum[:], cmat_d[:, 1, :])
        cload(urev[:], cmat_d[:, 2, :])
        cload(mask[:], cmat_d[:, 3, :])
        Bgain[id(gffn)] = cload(gffn[:], g_ffn_d.partition_broadcast(128))
        Bgain[id(gfin)] = cload(gfin[:], g_fin_d.partition_broadcast(128))
        P.add("dve", lambda e: e.memset(ones[:], 1.0), W=[sbuf_()])
        P.add("dve", lambda e: e.memset(ones_bf[:], 1.0), W=[sbuf_()])
        P.add("dve", lambda e: e.memset(glT[:], 1.0), W=[BglT])
        P.add("dve", lambda e: e.memset(k_out[:], 0.0), W=Bkout)
        P.add("dve", lambda e: e.memset(S[:], 0.0), W=[BS])
        P.add("dve", lambda e: e.memset(vT[:, :, 0:32], 0.0), W=[Bhist])

        def norm_part(hs, t, gb):
            i = nxt("xn", NXN)
            ht, Bht = h[hs][t], Bh[hs][t]
            P.add("act", lambda e: e.activation(out=xn_tm[i][:], in_=ht[:], func=AF.Square,
                                                accum_out=st[:, t, 0:1]), R=[Bht], W=[Bxn[i], Bst[t]])
            P.add("act", lambda e: e.activation(out=st[:, t, 1:2], in_=st[:, t, 0:1], func=AF.Ln,
                                                bias=RMS_EPS, scale=1.0 / D), R=[Bst[t]], W=[Bst[t]])
            P.add("act", lambda e: e.activation(out=st[:, t, 2:3], in_=st[:, t, 1:2], func=AF.Exp,
                                                scale=-0.5), R=[Bst[t]], W=[Bst[t]])
            P.add("dve", lambda e: e.scalar_tensor_tensor(out=xn_tm[i][:], in0=ht[:], scalar=st[:, t, 2:3],
                                                          in1=gb[:], op0=ALU.mult, op1=ALU.mult),
                  R=[Bht, Bst[t], Bgain[id(gb)]], W=[Bxn[i]])
            return i

        def tr_part(i, t, dstT, BdstT):
            pt, Bpt = psum()
            ptb = pt[:].bitcast(BF16).rearrange("p (k n) -> p k n", k=KC)

            def tr(e):
                ins = None
                for kc in range(KC):
                    ins = e.transpose(out=ptb[:, kc, :], in_=xn_tm[i][:, kc * 128:(kc + 1) * 128], identity=ident[:])
                return ins
            P.add("pe", tr, R=[Bxn[i], Bident], W=[Bpt])
            P.add("act", lambda e: e.activation(out=dstT[:, :, t * 128:(t + 1) * 128], in_=ptb, func=AF.Copy),
                  R=[Bpt], W=[BdstT])

        def norm_transpose(hs, t, gb, dstT, BdstT):
            tr_part(norm_part(hs, t, gb), t, dstT, BdstT)

        def wload(dst_ap, src_ap, Bslot):
            P.add("pool", lambda e: e.dma_start(out=dst_ap, in_=src_ap), W=[Bslot], dma=True)

        def w_cols(w, c0, ncols):
            return w[:, c0:c0 + ncols].rearrange("(kc p) n -> p kc n", p=128)

        def proj_fm(wv, Bwv, col0, ncols, srcT, BsrcT, tk=slice(0, T)):
            pt, Bpt = psum()

            def mm(e):
                ins = None
                for kc in range(KC):
                    ins = e.matmul(pt[0:ncols, tk], lhsT=wv[:, kc, col0:col0 + ncols], rhs=srcT[:, kc, tk],
                                   start=(kc == 0), stop=(kc == KC - 1))
                return ins
            P.add("pe", mm, R=[Bwv] + list(BsrcT), W=[Bpt])
            return pt, Bpt

        def wl(w, c0, ncols=256):
            s_, B_ = wslot()
            v = s_[:, 0:KC * ncols].rearrange("p (k n) -> p k n", k=KC)
            wload(v, w_cols(w, c0, ncols), B_)
            return v, B_

        def front(blk):
            hs, row0 = blk["hs"], blk["row0"]
            tiles = [NT - 1] if blk["pre"] else list(range(NT))
            for t in tiles:
                if blk["pre"]:
                    continue
                P.add("sp", lambda e, t=t: e.dma_start(out=h[hs][t][:], in_=xin[row0 + t * 128: row0 + (t + 1) * 128, :]),
                      W=[Bh[hs][t]], dma=True)
            if blk["first"] and not blk["pre"]:
                P.add("pool", lambda e: e.tensor_copy(out=S[:], in_=S0[:]), R=[BS0], W=[BS])
                P.add("pool", lambda e: e.tensor_copy(out=vT[:, :, 0:32], in_=hist0[:]), R=[Bhist0], W=[Bhist])
            blk["xn"] = {t: norm_part(hs, t, gmix) for t in tiles}

        def front_tr(blk):
            for t in sorted(blk["xn"]):
                tr_part(blk["xn"][t], t, xT, BxT[t])

        def rest(blk, nxt_blk):
            hs, out_row0, preamble = blk["hs"], blk["orow0"], blk["pre"]
            tiles = [NT - 1] if preamble else list(range(NT))
            tk = slice((NT - 1) * 128, T) if preamble else slice(0, T)
            if preamble:
                dbufs = []
                for j in range(NTAP):
                    for c in range(4):
                        b = Buf(f"dg{j}_{c}")
                        dbufs.append(b)
                        P.add("dve", lambda e, j=j, c=c: e.tensor_scalar(
                            out=diag[:, j, c, :], in0=identf[:], scalar1=convwT[:, c, j:j + 1], scalar2=None,
                            op0=ALU.mult), R=[Bconst], W=[b])
                P.add("dve", lambda e: e.memset(st[:, 0, 3:4], 0.0), R=dbufs, W=[Bdiag])

            pt, Bpt = proj_fm(wgl, Bconst, 0, 16, xT, BxT, tk)
            P.add("act", lambda e, pt=pt: e.activation(out=glT[0:16, tk], in_=pt[0:16, tk], func=AF.Copy),
                  R=[Bpt], W=[BglT])
            if "wgv" in blk:
                wg, wv = blk["wgv"]
            else:
                wg = [wl(w_in, 512 + 256 * i) for i in range(2)]
                wv = [wl(w_in, 256 * i) for i in range(2)]
            def b2_chunk(c):
                pg, Bpg = proj_fm(wg[c // 2][0], wg[c // 2][1], (c % 2) * 128, 128, xT, BxT, tk)
                tsg, Btsg = gtmp()
                P.add("act", lambda e: e.activation(out=tsg[:, tk], in_=pg[:, tk], func=AF.Sigmoid), R=[Bpg], W=[Btsg])
                pv, Bpv = proj_fm(wv[c // 2][0], wv[c // 2][1], (c % 2) * 128, 128, xT, BxT, tk)
                P.add("dve", lambda e: e.tensor_tensor(out=vT[:, c, 32 + tk.start:32 + tk.stop], in0=pv[:, tk],
                                                       in1=tsg[:, tk], op=ALU.mult),
                      R=[Bpv, Btsg], W=[BvT[c]])

            zst = {}

            def z_tile(t):
                tsl = slice(t * 128, (t + 1) * 128)
                pz, Bpz = psum()
                P.add("pe", lambda e: e.matmul(pz[:, 0:256], lhsT=glT[0:32, tsl], rhs=wg2[:, :], start=True, stop=True),
                      R=[BglT, Bconst], W=[Bpz])
                te, Bte = gtmp()
                P.add("act", lambda e: e.activation(out=te[:, 0:256], in_=pz[:, 0:256], func=AF.Exp, scale=-1.0),
                      R=[Bpz], W=[Bte])
                tsp, Btsp = gtmp()
                P.add("act", lambda e: e.activation(out=tsp[:, 0:256], in_=te[:, 0:256], func=AF.Ln, bias=1.0),
                      R=[Bte], W=[Btsp])
                zst[t] = (tsp, Btsp)

            def decay_tile(t):
                tsl = slice(t * 128, (t + 1) * 128)
                tsp, Btsp = zst[t]
                pr, Bpr = psum()
                P.add("pe", lambda e: e.matmul(pr[:, 0:256], lhsT=urev[:], rhs=tsp[:, 0:256], start=True, stop=True),
                      R=[Btsp, Bconst], W=[Bpr])
                P.add("act", lambda e: e.activation(out=exprev[:, t, :], in_=pr[:, 0:256], func=AF.Exp),
                      R=[Bpr], W=[Bexprev[t]])
                pc, Bpc = psum()

                def mmc(e):
                    ins = None
                    for hp in range(2):
                        ins = e.matmul(pc[:, hp * 128:(hp + 1) * 128], lhsT=tsp[:, hp * 128:(hp + 1) * 128],
                                       rhs=ucum[:], start=True, stop=True)
                    return ins
                P.add("pe", mmc, R=[Btsp, Bconst], W=[Bpc])
                P.add("act", lambda e: e.activation(
                    out=Eb[:, :, tsl], in_=pc[:, 0:256].rearrange("p (a n) -> p a n", a=2), func=AF.Exp),
                    R=[Bpc], W=[BE[t]])
                if not preamble:
                    P.add("act", lambda e: e.activation(
                        out=Einv[:, :, tsl], in_=pc[:, 0:256].rearrange("p (a n) -> p a n", a=2), func=AF.Exp,
                        scale=-1.0), R=[Bpc], W=[BEi[t]])

            if preamble:
                z_tile(NT - 1)
                for c in range(4):
                    b2_chunk(c)
                decay_tile(NT - 1)
            else:
                b2_chunk(0)
                z_tile(0)
                b2_chunk(1)
                z_tile(1)
                decay_tile(0)
                b2_chunk(2)
                z_tile(2)
                decay_tile(1)
                b2_chunk(3)
                z_tile(3)
                decay_tile(2)

            wk, Bwk = blk["wk"] if "wk" in blk else wl(w_in, 1280)
            if not preamble:
                wq, Bwq = wl(w_in, 1024)
                qk = {}
                for hp in range(2):
                    qk[hp] = (proj_fm(wq, Bwq, hp * 128, 128, xT, BxT), proj_fm(wk, Bwk, hp * 128, 128, xT, BxT))
                    if hp == 0:
                        decay_tile(3)
                    (pq, Bpq), (pk, Bpk) = qk[hp]
                    P.add("dve", lambda e, pq=pq, hp=hp: e.scalar_tensor_tensor(
                        out=q_in[:, hp, :], in0=pq[:], scalar=0.125, in1=Eb[:, hp, :], op0=ALU.mult, op1=ALU.mult),
                        R=[Bpq] + BE, W=[Bq[hp]])
                    P.add("dve", lambda e, pk=pk, hp=hp: e.tensor_tensor(
                        out=k_in[:, hp, :], in0=pk[:], in1=Einv[:, hp, :], op=ALU.mult),
                        R=[Bpk] + BEi, W=[Bk[hp]])

            if not preamble:
                p1, Bp1 = psum()
                p2, Bp2 = psum()
                cst = {}

                def conv_ops(c):
                    pcv, Bpcv = psum()

                    def mmconv(e, pcv=pcv, c=c):
                        ins = None
                        for j in range(NTAP):
                            ins = e.matmul(pcv[:], lhsT=diag[:, j, c, :], rhs=vT[:, c, 2 + j:2 + j + T],
                                           start=(j == 0), stop=(j == NTAP - 1))
                        return ins
                    P.add("pe", mmconv, R=[BvT[c], Bhist, Bdiag], W=[Bpcv])
                    P.add("act", lambda e: e.activation(out=ycv[:, c, :], in_=pcv[:], func=AF.Identity,
                                                        bias=cvec[:, 0, c:c + 1]),
                          R=[Bpcv, Bconst], W=Bycv[c])
                    iq = nxt("ysq", 2)
                    P.add("act", lambda e: e.activation(out=ysq[iq][:].bitcast(BF16)[:, 0:T], in_=pcv[:], func=AF.Square,
                                                        bias=cvec[:, 0, c:c + 1]),
                          R=[Bpcv, Bconst], W=Bysq[iq])
                    cst[c] = iq

                def ones_ops(c):
                    iq = cst[c]
                    P.add("pe", lambda e: e.matmul(p1[:], lhsT=ones[:], rhs=ycv[:, c, :], start=(c == 0), stop=(c == 3)),
                          R=Bycv[c] + [Bconst], W=[Bp1])
                    P.add("pe", lambda e: e.matmul(p2[:], lhsT=ones_bf[:], rhs=ysq[iq][:].bitcast(BF16)[:, 0:T],
                                                   start=(c == 0), stop=(c == 3)),
                          R=Bysq[iq] + [Bconst], W=[Bp2])

                conv_ops(0)
                conv_ops(1)
                ones_ops(0)
                conv_ops(2)
                ones_ops(1)
                conv_ops(3)
                ones_ops(2)
                ones_ops(3)
                P.add("dve", lambda e: e.tensor_scalar(out=mu[:], in0=p1[:], scalar1=1.0 / CC, scalar2=None, op0=ALU.mult),
                      R=[Bp1], W=Bmu)
                P.add("act", lambda e: e.activation(out=rsd[:], in_=p1[:], func=AF.Square, scale=1.0 / CC),
                      R=[Bp1], W=Brsd)
                P.add("dve", lambda e: e.scalar_tensor_tensor(out=rsd[:], in0=p2[:], scalar=1.0 / CC, in1=rsd[:],
                                                              op0=ALU.mult, op1=ALU.subtract), R=[Bp2] + Brsd, W=Brsd)

                def conv_normalize():
                    P.add("act", lambda e: e.activation(out=rsd[:], in_=rsd[:], func=AF.Ln, bias=LN_EPS), R=Brsd, W=Brsd)
                    P.add("act", lambda e: e.activation(out=rsd[:], in_=rsd[:], func=AF.Exp, scale=-0.5), R=Brsd, W=Brsd)
                    for c in range(4):
                        t1, Bt1 = gtmp()
                        P.add("dve", lambda e, t1=t1, c=c: e.tensor_tensor(out=t1[:], in0=ycv[:, c, :], in1=mu[:],
                                                                           op=ALU.subtract), R=Bycv[c] + Bmu, W=[Bt1])
                        P.add("dve", lambda e, t1=t1: e.tensor_tensor(out=t1[:], in0=t1[:], in1=rsd[:], op=ALU.mult),
                              R=[Bt1] + Brsd, W=[Bt1])
                        P.add("act", lambda e, t1=t1, c=c: e.activation(out=yT[:, c, :], in_=t1[:], func=AF.Silu,
                                                                        bias=cvec[:, 2, c:c + 1], scale=cvec[:, 1, c:c + 1]),
                              R=[Bt1, Bconst], W=[ByTc[c]])
                conv_normalize()

            wvt = [blk["wvt0"] if "wvt0" in blk else wl(w_in, 1536), wl(w_in, 1792)]
            for t in tiles:
                tsl = slice(t * 128, (t + 1) * 128)
                pv, Bpv = psum()

                def mmv(e, pv=pv, tsl=tsl):
                    ins = None
                    for hf in range(2):
                        for kc in range(KC):
                            ins = e.matmul(pv[:, hf * 256:(hf + 1) * 256], lhsT=xT[:, kc, tsl], rhs=wvt[hf][0][:, kc, :],
                                           start=(kc == 0), stop=(kc == KC - 1))
                    return ins
                P.add("pe", mmv, R=[wvt[0][1], wvt[1][1], BxT[t]], W=[Bpv])
                P.add("act", lambda e, pv=pv, t=t: e.activation(out=v_tm[:, t, :], in_=pv[:], func=AF.Copy),
                      R=[Bpv], W=[Bvtm[t]])
                pk, Bpk = psum()

                def mmk(e, pk=pk, tsl=tsl):
                    ins = None
                    for kc in range(KC):
                        ins = e.matmul(pk[:, 0:256], lhsT=xT[:, kc, tsl], rhs=wk[:, kc, :], start=(kc == 0),
                                       stop=(kc == KC - 1))
                    return ins
                P.add("pe", mmk, R=[Bwk, BxT[t]], W=[Bpk])
                ko = k_out[:, t, :].rearrange("p (a b n) -> p a b n", a=2, b=2)
                pk4 = pk[:, 0:256].rearrange("p (a b n) -> p a b n", a=2, b=2)
                er4 = exprev[:, t, :].rearrange("p (a b n) -> p a b n", a=2, b=2)
                for hh in range(2):
                    P.add("dve", lambda e, ko=ko, pk4=pk4, er4=er4, hh=hh: e.tensor_tensor(
                        out=ko[:, :, hh, hh * 64:hh * 64 + 64], in0=pk4[:, :, hh, :], in1=er4[:, :, hh, :],
                        op=ALU.mult), R=[Bpk, Bexprev[t]], W=[Bkout[t]])

            if not preamble:
                wgg = [wl(w_in, 2048 + 256 * i) for i in range(2)]
                for hh in range(4):
                    pg, Bpg = proj_fm(wgg[hh // 2][0], wgg[hh // 2][1], (hh % 2) * 128, 128, xT, BxT)
                    P.add("act", lambda e, pg=pg, hh=hh: e.activation(out=sgT[:, hh, :], in_=pg[:], func=AF.Silu),
                          R=[Bpg], W=[BsgT[hh]])

            scs = {}
            def d2_scores(t):
                tsl = slice(t * 128, (t + 1) * 128)
                pscs = [psum(), psum()]

                def mmsc(e, pscs=pscs, tsl=tsl):
                    ins = None
                    for hh in range(2):
                        for hp in range(2):
                            ins = e.matmul(pscs[hh][0][:, hp * 128:(hp + 1) * 128],
                                           lhsT=k_in[hh * 64:(hh + 1) * 64, hp, tsl],
                                           rhs=q_in[hh * 64:(hh + 1) * 64, hp, tsl], start=True, stop=True)
                    return ins
                P.add("pe", mmsc, R=Bq + Bk, W=[pscs[0][1], pscs[1][1]])
                isc = nxt("sc", 2)
                scv = sc_bf[isc][:].rearrange("p (a b) n -> p a b n", b=2)
                for hh in range(2):
                    P.add("dve", lambda e, pscs=pscs, scv=scv, hh=hh: e.tensor_tensor(
                        out=scv[:, :, hh, :], in0=pscs[hh][0][:, 0:256].rearrange("p (a n) -> p a n", a=2),
                        in1=mask[:].unsqueeze(1).to_broadcast([128, 2, 128]), op=ALU.mult),
                        R=[pscs[hh][1], Bconst], W=[Bsc[isc]])
                return isc

            if not preamble:
                scs[0] = d2_scores(0)
                scs[1] = d2_scores(1)
            kvb = [psum() for _ in range(4)]

            def kv_ap(t, n, hp):
                return kvb[n * 2 + t // 2][0][:, (t % 2) * 256 + hp * 128:(t % 2) * 256 + (hp + 1) * 128]

            for n in range(2):
                def mmkv(e, n=n):
                    ins = None
                    rows = slice(n * 64, (n + 1) * 64)
                    for t in tiles:
                        for hp in range(2):
                            for hh in range(2):
                                hd = 2 * hp + hh
                                ins = e.matmul(kv_ap(t, n, hp), lhsT=k_out[rows, t, hd * 128:(hd + 1) * 128],
                                               rhs=v_tm[rows, t, hd * 128:(hd + 1) * 128], start=(hh == 0), stop=(hh == 1))
                    return ins
                P.add("pe", mmkv, R=Bkout + Bvtm, W=[kvb[n * 2][1], kvb[n * 2 + 1][1]])
            for m in range(2 * tiles[0], 2 * NT):
                t, n = divmod(m, 2)
                if not preamble:
                    P.add("dve", lambda e, m=m: e.tensor_copy(out=S_snap[:, m, :, :], in_=S[:]),
                          R=[BS], W=[Bsnap[m]])
                lastc = t * 128 + n * 64 + 63
                for hp in range(2):
                    P.add("dve", lambda e, t=t, n=n, hp=hp, lastc=lastc: e.scalar_tensor_tensor(
                        out=S[:, hp, :], in0=S[:, hp, :], scalar=Eb[:, hp, lastc:lastc + 1],
                        in1=kv_ap(t, n, hp), op0=ALU.mult, op1=ALU.add),
                        R=[BS, BE[t], kvb[n * 2 + t // 2][1]], W=[BS])

            if not preamble:
                wo = [wl(w_out, 256 * i) for i in range(4)]

                def wout_tile(t):
                    tsl = slice(t * 128, (t + 1) * 128)
                    for cb in range(2):
                        pt, Bpt = psum()

                        def mmo2(e, pt=pt, tsl=tsl, cb=cb):
                            ins = None
                            for hf in range(2):
                                for kc in range(KC):
                                    ins = e.matmul(pt[:, hf * 256:(hf + 1) * 256], lhsT=yT[:, kc, tsl],
                                                   rhs=wo[2 * cb + hf][0][:, kc, :], start=(kc == 0), stop=(kc == KC - 1))
                            return ins
                        P.add("pe", mmo2, R=[wo[2 * cb][1], wo[2 * cb + 1][1], ByTg[t]] + ByTc, W=[Bpt])
                        P.add("dve", lambda e, pt=pt, t=t, cb=cb: e.tensor_tensor(
                            out=h[hs][t][:, cb * 512:(cb + 1) * 512], in0=h[hs][t][:, cb * 512:(cb + 1) * 512], in1=pt[:],
                            op=ALU.add), R=[Bpt, Bh[hs][t]], W=[Bh[hs][t]])


                def d2_out(t, isc):
                    tsl = slice(t * 128, (t + 1) * 128)
                    po, Bpo = psum()
                    for n in range(2):
                        m = 2 * t + n

                        def mmo(e, po=po, isc=isc, n=n, t=t, m=m):
                            ins = None
                            for hd in range(4):
                                hp, hh = hd // 2, hd % 2
                                oc = slice(hd * 128 + n * 64, hd * 128 + n * 64 + 64)
                                e.matmul(po[:, oc], lhsT=v_tm[:, t, hd * 128:(hd + 1) * 128],
                                         rhs=sc_bf[isc][:, hd, n * 64:(n + 1) * 64], start=True, stop=False)
                                ins = e.matmul(po[:, oc], lhsT=S_snap[hh * 64:(hh + 1) * 64, m, hp, :],
                                               rhs=q_in[hh * 64:(hh + 1) * 64, hp, t * 128 + n * 64: t * 128 + n * 64 + 64],
                                               start=False, stop=True)
                            return ins
                        P.add("pe", mmo, R=[Bvtm[t], Bsc[isc], Bsnap[m]] + Bq, W=[Bpo])
                    return po, Bpo

                def d2_norm(t, po, Bpo):
                    tsl = slice(t * 128, (t + 1) * 128)
                    tq, Btq = gtmp()
                    P.add("act", lambda e, po=po, tq=tq: e.activation(out=tq[:].bitcast(BF16)[:, 0:T], in_=po[:], func=AF.Square),
                          R=[Bpo], W=[Btq])
                    pss, Bpss = psum()
                    P.add("pe", lambda e, pss=pss, tq=tq: e.matmul(pss[:], lhsT=ones_bf[:], rhs=tq[:].bitcast(BF16)[:, 0:T],
                                                                   start=True, stop=True),
                          R=[Btq, Bconst], W=[Bpss])
                    tr_, Btr = gtmp()
                    P.add("act", lambda e, pss=pss, tr_=tr_: e.activation(out=tr_[:], in_=pss[:], func=AF.Ln,
                                                                          bias=RMS_EPS, scale=1.0 / 128),
                          R=[Bpss], W=[Btr])
                    P.add("act", lambda e, tr_=tr_: e.activation(out=tr_[:], in_=tr_[:], func=AF.Exp, scale=-0.5),
                          R=[Btr], W=[Btr])
                    P.add("dve", lambda e, po=po, tr_=tr_: e.tensor_tensor(out=tr_[:], in0=po[:], in1=tr_[:], op=ALU.mult),
                          R=[Bpo, Btr], W=[Btr])
                    P.add("dve", lambda e, tr_=tr_, tsl=tsl: e.scalar_tensor_tensor(
                        out=yT[:, 4:8, tsl], in0=tr_[:].rearrange("p (a n) -> p a n", a=4), scalar=gng[:, 0:1],
                        in1=sgT[:, :, tsl], op0=ALU.mult, op1=ALU.mult),
                        R=[Btr, Bconst] + BsgT, W=[ByTg[t]])


                def nt(t):
                    norm_transpose(hs, t, gffn, xT, BxT[t])

                o0 = d2_out(0, scs[0])
                scs[2] = d2_scores(2)
                o1 = d2_out(1, scs[1])
                d2_norm(0, *o0)
                scs[3] = d2_scores(3)
                o2 = d2_out(2, scs[2])
                d2_norm(1, *o1)
                wout_tile(0)
                o3 = d2_out(3, scs[3])
                d2_norm(2, *o2)
                wout_tile(1)
                nt(0)
                d2_norm(3, *o3)
                wout_tile(2)
                nt(1)
                wout_tile(3)
                nt(2)
                nt(3)

            P.add("pool", lambda e: e.tensor_copy(out=vT[:, :, 0:32], in_=vT[:, :, T:T + 32]), R=BvT, W=[Bhist])
            if preamble:
                P.add("pool", lambda e: e.tensor_copy(out=S0[:], in_=S[:]), R=[BS], W=[BS0])
                P.add("pool", lambda e: e.tensor_copy(out=hist0[:], in_=vT[:, :, 0:32]), R=[Bhist], W=[Bhist0])
                if nxt_blk is not None:
                    front(nxt_blk)
                    front_tr(nxt_blk)
                return

            if nxt_blk is not None:
                front(nxt_blk)
            for j0 in range(0, NJ, 2):
                wgv, Bwg_ = wl(w_fg, j0 * 128)
                wuv, Bwu_ = wl(w_fu, j0 * 128)
                for jj in range(2):
                    j = j0 + jj
                    pg, Bpg = proj_fm(wgv, Bwg_, jj * 128, 128, xT, BxT)
                    pu, Bpu = proj_fm(wuv, Bwu_, jj * 128, 128, xT, BxT)
                    ts_, Bts = gtmp()
                    P.add("act", lambda e, pg=pg, ts_=ts_: e.activation(out=ts_[:], in_=pg[:], func=AF.Silu),
                          R=[Bpg], W=[Bts])
                    P.add("dve", lambda e, pu=pu, ts_=ts_, j=j: e.tensor_tensor(out=actT[:, j, :], in0=pu[:], in1=ts_[:],
                                                                               op=ALU.mult), R=[Bpu, Bts], W=[Bact[j]])
            if nxt_blk is not None:
                front_tr(nxt_blk)
            for cb in range(2):
                acc = [psum() for _ in range(NT)]
                for j0 in range(0, NJ, 2):
                    wd_, Bwd = wslot()
                    wd = wd_[:, 0:1024].rearrange("p (j n) -> p j n", j=2)
                    wload(wd, w_fd[j0 * 128:(j0 + 2) * 128, cb * 512:(cb + 1) * 512].rearrange("(j p) n -> p j n", p=128), Bwd)

                    def mmd(e, wd=wd, j0=j0, acc=acc):
                        ins = None
                        for jj in range(2):
                            j = j0 + jj
                            for t in range(NT):
                                ins = e.matmul(acc[t][0][:], lhsT=actT[:, j, t * 128:(t + 1) * 128], rhs=wd[:, jj, :],
                                               start=(j == 0), stop=(j == NJ - 1))
                        return ins
                    P.add("pe", mmd, R=[Bwd] + Bact[j0:j0 + 2], W=[a[1] for a in acc])
                    if cb == 1 and j0 == NJ - 2 and nxt_blk is not None:
                        nxt_blk["wgv"] = ([wl(w_in, 512 + 256 * i) for i in range(2)],
                                          [wl(w_in, 256 * i) for i in range(2)])
                for t in range(NT):
                    P.add("dve", lambda e, t=t, cb=cb, a=acc[t][0]: e.tensor_tensor(
                        out=h[hs][t][:, cb * 512:(cb + 1) * 512], in0=h[hs][t][:, cb * 512:(cb + 1) * 512], in1=a[:],
                        op=ALU.add), R=[acc[t][1], Bh[hs][t]], W=[Bh[hs][t]])
            for t in range(NT):
                tj, Btj = gtmp()
                ht, Bht = h[hs][t], Bh[hs][t]
                P.add("act", lambda e, t=t, tj=tj, ht=ht: e.activation(out=tj[:].bitcast(BF16), in_=ht[:], func=AF.Square,
                                                                       accum_out=st[:, t, 4:5]),
                      R=[Bht], W=[Btj, Bst2[t]])
                P.add("act", lambda e, t=t: e.activation(out=st[:, t, 5:6], in_=st[:, t, 4:5], func=AF.Ln,
                                                         bias=RMS_EPS, scale=1.0 / D), R=[Bst2[t]], W=[Bst2[t]])
                P.add("act", lambda e, t=t: e.activation(out=st[:, t, 6:7], in_=st[:, t, 5:6], func=AF.Exp,
                                                         scale=-0.5), R=[Bst2[t]], W=[Bst2[t]])
                P.add("dve", lambda e, t=t, ht=ht: e.scalar_tensor_tensor(out=ht[:], in0=ht[:], scalar=st[:, t, 6:7],
                                                                          in1=gfin[:], op0=ALU.mult, op1=ALU.mult),
                      R=[Bht, Bst2[t], Bconst], W=[Bht])
                P.add("sp", lambda e, t=t, ht=ht: e.dma_start(out=out[out_row0 + t * 128: out_row0 + (t + 1) * 128, :],
                                                              in_=ht[:]),
                      R=[Bht], dma=True, dkey=Bht.name)

        blocks = [dict(hs=0, row0=0, orow0=0, first=False, pre=True, **pre_w)]
        for s_i in range(nseq):
            for b in range(nblk_seq):
                r = s_i * seqlen + b * T
                blocks.append(dict(hs=len(blocks) % 2, row0=T + r, orow0=r, first=(b == 0), pre=False))
        front(blocks[0])
        front_tr(blocks[0])
        P.add("dve", lambda e: e.memset(st[:, 0, 7:8], 0.0), R=list(setup_bufs), W=[Bconst])
        for bi, blk in enumerate(blocks):
            rest(blk, blocks[bi + 1] if bi + 1 < len(blocks) else None)
        P.add("sp", None, W=Bh[0] + Bh[1])
        P.finalize()
        dkeys = sorted(P.dcount.keys())
        esem = {e: es.enter_context(nc.semaphore(f"sem_{e}")) for e in Prog.ENGS}
        dsem = {k: es.enter_context(nc.semaphore(f"dsem_{k}")) for k in dkeys}
        with nc.Block() as blk:
            @blk.tensor
            def _(e):
                P.emit("pe", e, esem, dsem)

            @blk.scalar
            def _(e):
                P.emit("act", e, esem, dsem)

            @blk.vector
            def _(e):
                P.emit("dve", e, esem, dsem)

            @blk.gpsimd
            def _(e):
                P.emit("pool", e, esem, dsem)

            @blk.sync
            def _(e):
                P.emit("sp", e, esem, dsem)
    return nc


def make_in_maps(inputs, nseq, seqlen, n_cores):
    x = np.ascontiguousarray(inputs["x"], dtype=np.float32)
    meta = np.asarray(inputs["meta_tokens"], dtype=np.float32)
    pre = np.zeros((T, D), np.float32)
    pre[T - meta.shape[0]:] = meta
    conv_wT = np.ascontiguousarray(
        np.asarray(inputs["conv_w"][0], np.float32).T.reshape(4, 128, NTAP).transpose(1, 0, 2))
    vecs = np.stack([np.asarray(inputs[k][0], np.float32).reshape(4, 128).T
                     for k in ("conv_b", "conv_ln_g", "conv_ln_b")], axis=1)
    common = {
        "w_in": np.ascontiguousarray(inputs["w_in"][0], dtype=np.float32),
        "w_out": np.ascontiguousarray(inputs["w_out"][0], dtype=np.float32),
        "w_ffn_gate": np.ascontiguousarray(inputs["w_ffn_gate"][0], dtype=np.float32),
        "w_ffn_up": np.ascontiguousarray(inputs["w_ffn_up"][0], dtype=np.float32),
        "w_ffn_down": np.ascontiguousarray(inputs["w_ffn_down"][0], dtype=np.float32),
        "norm_mix_g": np.asarray(inputs["norm_mix_g"], np.float32).reshape(1, D),
        "norm_ffn_g": np.asarray(inputs["norm_ffn_g"], np.float32).reshape(1, D),
        "norm_final_g": np.asarray(inputs["norm_final_g"], np.float32).reshape(1, D),
        "conv_wT": conv_wT,
        "conv_vecs": np.ascontiguousarray(vecs),
        "gla_w_gate2": np.ascontiguousarray(inputs["gla_w_gate2"][0], dtype=np.float32),
        "gla_gate_b": np.asarray(inputs["gla_gate_b"], np.float32).reshape(1, 256),
        "gla_norm_g": np.asarray(inputs["gla_norm_g"], np.float32).reshape(128, 1),
    }
    jj, ii = np.meshgrid(np.arange(128), np.arange(128), indexing="ij")
    same = (jj // 64) == (ii // 64)
    cm = np.zeros((128, 4, 128), np.float32)
    cm[:, 0, :] = (jj == ii)
    cm[:, 1, :] = np.where(same & (jj <= ii), -1.0 / 16, 0.0)
    cm[:, 2, :] = np.where(same & (jj > ii), -1.0 / 16, 0.0)
    cm[:, 3, :] = (same & (jj <= ii))
    common["c_mats"] = cm
    maps = []
    for c in range(n_cores):
        xc = x[c * nseq:(c + 1) * nseq].reshape(nseq * seqlen, D)
        m = dict(common)
        m["xin"] = np.concatenate([pre, xc], axis=0)
        maps.append(m)
    return maps


def kernel(**inputs):
    x = inputs["x"]
    bsz, seqlen, _ = x.shape
    nseq = bsz // N_CORES
    nc = build_program(nseq, seqlen)
    in_maps = make_in_maps(inputs, nseq, seqlen, N_CORES)
    res = run_bass_kernel_spmd(nc, in_maps, core_ids=list(range(N_CORES)))
    outs = [np.asarray(r["out"], dtype=np.float32).reshape(nseq, seqlen, D) for r in res.results]
    return np.concatenate(outs, axis=0)
```

```python
import contextlib
import os
import numpy as np
import concourse.bass as bass
import concourse.mybir as mybir
from concourse.bass_utils import run_bass_kernel_spmd

F32 = mybir.dt.float32
BF16 = mybir.dt.bfloat16
AF = mybir.ActivationFunctionType
ALU = mybir.AluOpType

D = 1024
KC = 8
DIN = 2576
CC = 512
DFF = 2816
NJ = 22
T = 512
NT = 4
NTAP = 31
N_CORES = 8
RMS_EPS = 1e-6
LN_EPS = 1e-5


class Buf:
    def __init__(self, name, psum=False):
        self.name = name
        self.lw = None
        self.rd = []
        self.psum = psum


class Op:
    __slots__ = ("eng", "fn", "deps", "sig", "ticket", "dma", "dkey", "dval", "gi")

    def __init__(self, eng, fn, dma, dkey):
        self.eng = eng
        self.fn = fn
        self.deps = []
        self.sig = False
        self.ticket = 0
        self.dma = dma
        self.dkey = dkey
        self.dval = 0
        self.gi = 0


class Prog:
    ENGS = ("pe", "act", "dve", "pool", "sp")

    def __init__(self):
        self.ops = {e: [] for e in self.ENGS}
        self.n = 0
        self.dcount = {}

    def add(self, eng, fn, R=(), W=(), dma=False, dkey=None):
        if dma and dkey is None:
            dkey = W[0].name if W else R[0].name
        op = Op(eng, fn, dma, dkey)
        op.gi = self.n
        self.n += 1
        deps = {}
        for b in R:
            if b.lw is not None:
                deps[id(b.lw)] = b.lw
            if b.psum:
                for r in b.rd:
                    if r.eng != eng:
                        deps[id(r)] = r
        for b in W:
            if b.lw is not None:
                deps[id(b.lw)] = b.lw
            for r in b.rd:
                deps[id(r)] = r
        deps.pop(id(op), None)
        op.deps = list(deps.values())
        for b in R:
            b.rd.append(op)
        for b in W:
            b.lw = op
            b.rd = []
        if dma:
            self.dcount[dkey] = self.dcount.get(dkey, 0) + 16
            op.dval = self.dcount[dkey]
        self.ops[eng].append(op)
        return op

    @staticmethod
    def _needs_wait(a, b):
        if a.dma:
            return True
        if a.eng == "pe" and b.eng == "pe":
            return False
        return True

    def finalize(self):
        for e in self.ENGS:
            for b in self.ops[e]:
                for a in b.deps:
                    if not a.dma and self._needs_wait(a, b):
                        a.sig = True
        for e in self.ENGS:
            c = 0
            for a in self.ops[e]:
                if a.sig and not a.dma:
                    c += 1
                    a.ticket = c

    def emit(self, eng_name, eng, esem, dsem):
        seen = {}
        for b in self.ops[eng_name]:
            waits = {}
            for a in b.deps:
                if not self._needs_wait(a, b):
                    continue
                if a.dma:
                    s, v = dsem[a.dkey], (self.dcount[a.dkey] if a.dkey == "const" else a.dval)
                else:
                    s, v = esem[a.eng], a.ticket
                k = id(s)
                if k not in waits or waits[k][1] < v:
                    waits[k] = (s, v)
            for k, (s, v) in waits.items():
                if seen.get(k, 0) >= v:
                    continue
                seen[k] = v
                eng.wait_ge(s, v)
            if b.fn is None:
                continue
            ins = b.fn(eng)
            if b.dma:
                ins.then_inc(dsem[b.dkey], 16)
            elif b.sig:
                ins.then_inc(esem[eng_name], 1)


def build_program(nseq, seqlen):
    assert seqlen % T == 0
    nblk_seq = seqlen // T
    ntok = nseq * seqlen
    nc = bass.Bass("TRN2", target_bir_lowering=False)
    P = Prog()

    def din(name, shape, dt=F32):
        return nc.dram_tensor(name, list(shape), dt, kind="ExternalInput").ap()

    xin = din("xin", [T + ntok, D])
    w_in = din("w_in", [D, DIN])
    w_out = din("w_out", [D, D])
    w_fg = din("w_ffn_gate", [D, DFF])
    w_fu = din("w_ffn_up", [D, DFF])
    w_fd = din("w_ffn_down", [DFF, D])
    g_mix_d = din("norm_mix_g", [1, D])
    g_ffn_d = din("norm_ffn_g", [1, D])
    g_fin_d = din("norm_final_g", [1, D])
    convwT_d = din("conv_wT", [128, 4, NTAP])
    cvec_d = din("conv_vecs", [128, 3, 4])
    wg2_d = din("gla_w_gate2", [16, 256])
    gb_d = din("gla_gate_b", [1, 256])
    gng_d = din("gla_norm_g", [128, 1])
    out = nc.dram_tensor("out", [ntok, D], F32, kind="ExternalOutput").ap()

    es = contextlib.ExitStack()
    with es:
        def sb(name, shape, dt=F32):
            return es.enter_context(nc.sbuf_tensor(name, list(shape), dt))

        h = [[sb(f"h{s_}_{t}", [128, D]) for t in range(NT)] for s_ in range(2)]
        st = sb("st", [128, NT, 8])
        NXN = 4
        xn_tm = [sb(f"xn_tm{i}", [128, D], BF16) for i in range(NXN)]
        xT = sb("xT", [128, KC, T], BF16)
        yT = sb("yT", [128, KC, T], BF16)
        vT = sb("vT", [128, 4, 32 + T], BF16)
        hist0 = sb("hist0", [128, 4, 32], BF16)
        NTMP = 5
        tmp = [sb(f"tmp{i}", [128, T]) for i in range(NTMP)]
        glT = sb("glT", [32, T])
        exprev = sb("exprev", [128, NT, 256])
        Eb = sb("Eb", [128, 2, T])
        Einv = sb("Einv", [128, 2, T])
        q_in = sb("q_in", [128, 2, T], BF16)
        k_in = sb("k_in", [128, 2, T], BF16)
        v_tm = sb("v_tm", [128, NT, 512], BF16)
        k_out = sb("k_out", [128, NT, 512], BF16)
        sgT = sb("sgT", [128, 4, T])
        sc_bf = [sb(f"sc_bf{i}", [128, 4, 128], BF16) for i in range(2)]
        S = sb("S", [128, 2, 128])
        S0 = sb("S0", [128, 2, 128])
        S_snap = sb("S_snap", [128, 2 * NT, 2, 128], BF16)
        actT = sb("actT", [128, NJ, T], BF16)

        def act_f32(j0, nch):
            return actT[:, j0:j0 + nch, :].rearrange("p a n -> p (a n)").bitcast(F32)
        ycv = act_f32(14, 8).rearrange("p (c n) -> p c n", c=4)
        ysq = [act_f32(10 + 2 * i, 2) for i in range(2)]
        mu = act_f32(8, 2)
        rsd = act_f32(6, 2)
        NW = 6
        wring = [sb(f"wring{i}", [128, 2048], BF16) for i in range(NW)]
        gmix = sb("gmix", [128, D])
        gffn = sb("gffn", [128, D])
        gfin = sb("gfin", [128, D])
        diag = sb("diag", [128, NTAP, 4, 128], BF16)
        identf = sb("identf", [128, 128])
        ident = sb("ident", [128, 128], BF16)
        ones = sb("ones", [128, 128])
        ones_bf = sb("ones_bf", [128, 128], BF16)
        urev = sb("urev", [128, 128])
        ucum = sb("ucum", [128, 128])
        mask = sb("mask", [128, 128])
        wg2 = sb("wg2", [32, 256])
        wgl = sb("wgl", [128, KC, 16], BF16)
        convwT = sb("convwT", [128, 4, NTAP])
        cvec = sb("cvec", [128, 3, 4])
        gng = sb("gng", [128, 1])

        NPS = 8
        ps = [es.enter_context(nc.psum_tensor(f"ps{i}", [128, 512], F32)) for i in range(NPS)]

        Bh = [[Buf(f"h{s_}_{t}") for t in range(NT)] for s_ in range(2)]
        Bst = [Buf(f"st{t}") for t in range(NT)]
        Bst2 = [Buf(f"stf{t}") for t in range(NT)]
        Bxn = [Buf(f"xn{i}") for i in range(NXN)]
        BxT = [Buf(f"xT{t}") for t in range(NT)]
        ByTc = [Buf(f"yTc{c}") for c in range(4)]
        ByTg = [Buf(f"yTg{t}") for t in range(NT)]
        BvT = [Buf(f"vT{c}") for c in range(4)]
        Bhist = Buf("hist")
        Bhist0 = Buf("hist0")
        Btmp = [Buf(f"tmp{i}") for i in range(NTMP)]
        BglT = Buf("glT")
        Bexprev = [Buf(f"exprev{t}") for t in range(NT)]
        BE = [Buf(f"E{t}") for t in range(NT)]
        BEi = [Buf(f"Ei{t}") for t in range(NT)]
        Bq = [Buf(f"q{hp}") for hp in range(2)]
        Bk = [Buf(f"k{hp}") for hp in range(2)]
        Bvtm = [Buf(f"vtm{t}") for t in range(NT)]
        Bkout = [Buf(f"kout{t}") for t in range(NT)]
        BsgT = [Buf(f"sgT{hh}") for hh in range(4)]
        Bsc = [Buf(f"sc{i}") for i in range(2)]
        BS = Buf("S")
        BS0 = Buf("S0")
        Bsnap = [Buf(f"snap{m}") for m in range(2 * NT)]
        Bdiag = Buf("diag")
        Bact = [Buf(f"act{j}") for j in range(NJ)]
        Bycv = [[Bact[14 + 2 * c], Bact[15 + 2 * c]] for c in range(4)]
        Bysq = [[Bact[10 + 2 * i], Bact[11 + 2 * i]] for i in range(2)]
        Bmu = [Bact[8], Bact[9]]
        Brsd = [Bact[6], Bact[7]]
        Bw = [Buf(f"w{i}") for i in range(NW)]
        Bps = [Buf(f"ps{i}", psum=True) for i in range(NPS)]
        Bconst = Buf("const")

        rr = {"ps": 0, "tmp": 0, "w": 0, "xn": 0, "ysq": 0, "sc": 0}

        def nxt(kind, n):
            i = rr[kind]
            rr[kind] = (i + 1) % n
            return i

        def psum():
            i = nxt("ps", NPS)
            return ps[i], Bps[i]

        def gtmp():
            i = nxt("tmp", NTMP)
            return tmp[i], Btmp[i]

        def wslot():
            i = nxt("w", NW)
            return wring[i], Bw[i]

        setup_bufs = []

        def sbuf_():
            b = Buf(f"setup{len(setup_bufs)}")
            setup_bufs.append(b)
            return b

        def cload(dst_ap, src_ap, eng="sp", after=(), dkey=None):
            b = sbuf_()
            P.add(eng, lambda e: e.dma_start(out=dst_ap, in_=src_ap), R=list(after), W=[b], dma=True,
                  dkey=dkey or ("const" if eng == "sp" else "constp"))
            return b

        Bid = sbuf_()
        P.add("pool", lambda e: e.memset(identf[:], 0.0), W=[Bid])
        P.add("pool", lambda e: e.affine_select(out=identf[:], in_=identf[:], pattern=[[-1, 128]],
                                                compare_op=ALU.not_equal, fill=1.0, base=0,
                                                channel_multiplier=1), R=[Bid], W=[Bid])
        cload(wgl[:], w_in[:, 2560:2576].rearrange("(kc p) n -> p kc n", p=128), eng="pool")
        pre_row = (NT - 1) * 128
        P.add("sp", lambda e: e.dma_start(out=h[0][NT - 1][:], in_=xin[pre_row:pre_row + 128, :]),
              W=[Bh[0][NT - 1]], dma=True)

        def early_wl(c0):
            s_, B_ = wslot()
            v = s_[:, 0:KC * 256].rearrange("p (k n) -> p k n", k=KC)
            P.add("pool", lambda e: e.dma_start(out=v, in_=w_in[:, c0:c0 + 256].rearrange("(kc p) n -> p kc n", p=128)),
                  W=[B_], dma=True)
            return v, B_
        pre_w = dict(wgv=([early_wl(512), early_wl(768)], [early_wl(0), early_wl(256)]),
                     wk=early_wl(1280), wvt0=early_wl(1536))


        Bgain = {id(gmix): cload(gmix[:], g_mix_d.partition_broadcast(128), dkey="c_gmix")}
        Bident = sbuf_()
        P.add("dve", lambda e: e.tensor_copy(out=ident[:], in_=identf[:]), R=[Bid], W=[Bident])
        Bwg2m = sbuf_()
        P.add("dve", lambda e: e.memset(wg2[:], 0.0), W=[Bwg2m])
        BconvwT = cload(convwT[:], convwT_d)
        cload(cvec[:], cvec_d)
        cload(gng[:], gng_d)
        cload(wg2[0:16, :], wg2_d, after=[Bwg2m])
        cload(wg2[16:17, :], gb_d, after=[Bwg2m])

        def tri(dst, val, keep_le):
            b = sbuf_()
            P.add("pool", lambda e: e.memset(dst[:], val), W=[b])
            if keep_le:
                P.add("pool", lambda e: e.affine_select(out=dst[:], in_=dst[:], pattern=[[1, 128]],
                                                        compare_op=ALU.is_ge, fill=0.0, base=0,
                                                        channel_multiplier=-1), R=[b], W=[b])
            else:
                P.add("pool", lambda e: e.affine_select(out=dst[:], in_=dst[:], pattern=[[-1, 128]],
                                                        compare_op=ALU.is_ge, fill=0.0, base=-1,
                                                        channel_multiplier=1), R=[b], W=[b])
            P.add("pool", lambda e: e.memset(dst[0:64, 64:128], 0.0), R=[b], W=[b])
            P.add("pool", lambda e: e.memset(dst[64:128, 0:64], 0.0), R=[b], W=[b])

        tri(ucum, -1.0 / 16, True)
        tri(urev, -1.0 / 16, False)
        tri(mask, 1.0, True)
        Bgain[id(gffn)] = cload(gffn[:], g_ffn_d.partition_broadcast(128))
        Bgain[id(gfin)] = cload(gfin[:], g_fin_d.partition_broadcast(128))
        P.add("dve", lambda e: e.memset(ones[:], 1.0), W=[sbuf_()])
        P.add("dve", lambda e: e.memset(ones_bf[:], 1.0), W=[sbuf_()])
        P.add("dve", lambda e: e.memset(glT[:], 1.0), W=[BglT])
        P.add("dve", lambda e: e.memset(k_out[:], 0.0), W=Bkout)
        P.add("dve", lambda e: e.memset(S[:], 0.0), W=[BS])
        P.add("dve", lambda e: e.memset(vT[:, :, 0:32], 0.0), W=[Bhist])

        def norm_part(hs, t, gb):
            i = nxt("xn", NXN)
            ht, Bht = h[hs][t], Bh[hs][t]
            P.add("act", lambda e: e.activation(out=xn_tm[i][:], in_=ht[:], func=AF.Square,
                                                accum_out=st[:, t, 0:1]), R=[Bht], W=[Bxn[i], Bst[t]])
            P.add("act", lambda e: e.activation(out=st[:, t, 1:2], in_=st[:, t, 0:1], func=AF.Ln,
                                                bias=RMS_EPS, scale=1.0 / D), R=[Bst[t]], W=[Bst[t]])
            P.add("act", lambda e: e.activation(out=st[:, t, 2:3], in_=st[:, t, 1:2], func=AF.Exp,
                                                scale=-0.5), R=[Bst[t]], W=[Bst[t]])
            P.add("dve", lambda e: e.scalar_tensor_tensor(out=xn_tm[i][:], in0=ht[:], scalar=st[:, t, 2:3],
                                                          in1=gb[:], op0=ALU.mult, op1=ALU.mult),
                  R=[Bht, Bst[t], Bgain[id(gb)]], W=[Bxn[i]])
            return i

        def tr_part(i, t, dstT, BdstT):
            pt, Bpt = psum()
            ptb = pt[:].bitcast(BF16).rearrange("p (k n) -> p k n", k=KC)

            def tr(e):
                ins = None
                for kc in range(KC):
                    ins = e.transpose(out=ptb[:, kc, :], in_=xn_tm[i][:, kc * 128:(kc + 1) * 128], identity=ident[:])
                return ins
            P.add("pe", tr, R=[Bxn[i], Bident], W=[Bpt])
            P.add("act", lambda e: e.activation(out=dstT[:, :, t * 128:(t + 1) * 128], in_=ptb, func=AF.Copy),
                  R=[Bpt], W=[BdstT])

        def norm_transpose(hs, t, gb, dstT, BdstT):
            tr_part(norm_part(hs, t, gb), t, dstT, BdstT)

        def wload(dst_ap, src_ap, Bslot):
            P.add("pool", lambda e: e.dma_start(out=dst_ap, in_=src_ap), W=[Bslot], dma=True)

        def w_cols(w, c0, ncols):
            return w[:, c0:c0 + ncols].rearrange("(kc p) n -> p kc n", p=128)

        def proj_fm(wv, Bwv, col0, ncols, srcT, BsrcT, tk=slice(0, T)):
            pt, Bpt = psum()

            def mm(e):
                ins = None
                for kc in range(KC):
                    ins = e.matmul(pt[0:ncols, tk], lhsT=wv[:, kc, col0:col0 + ncols], rhs=srcT[:, kc, tk],
                                   start=(kc == 0), stop=(kc == KC - 1))
                return ins
            P.add("pe", mm, R=[Bwv] + list(BsrcT), W=[Bpt])
            return pt, Bpt

        def wl(w, c0, ncols=256):
            s_, B_ = wslot()
            v = s_[:, 0:KC * ncols].rearrange("p (k n) -> p k n", k=KC)
            wload(v, w_cols(w, c0, ncols), B_)
            return v, B_

        def front(blk):
            hs, row0 = blk["hs"], blk["row0"]
            tiles = [NT - 1] if blk["pre"] else list(range(NT))
            for t in tiles:
                if blk["pre"]:
                    continue
                P.add("sp", lambda e, t=t: e.dma_start(out=h[hs][t][:], in_=xin[row0 + t * 128: row0 + (t + 1) * 128, :]),
                      W=[Bh[hs][t]], dma=True)
            if blk["first"] and not blk["pre"]:
                P.add("pool", lambda e: e.tensor_copy(out=S[:], in_=S0[:]), R=[BS0], W=[BS])
                P.add("pool", lambda e: e.tensor_copy(out=vT[:, :, 0:32], in_=hist0[:]), R=[Bhist0], W=[Bhist])
            blk["xn"] = {t: norm_part(hs, t, gmix) for t in tiles}

        def front_tr(blk):
            for t in sorted(blk["xn"]):
                tr_part(blk["xn"][t], t, xT, BxT[t])

        def rest(blk, nxt_blk):
            hs, out_row0, preamble = blk["hs"], blk["orow0"], blk["pre"]
            tiles = [NT - 1] if preamble else list(range(NT))
            tk = slice((NT - 1) * 128, T) if preamble else slice(0, T)
            if preamble:
                dbufs = []
                for j in range(NTAP):
                    for c in range(4):
                        b = Buf(f"dg{j}_{c}")
                        dbufs.append(b)
                        P.add("dve", lambda e, j=j, c=c: e.tensor_scalar(
                            out=diag[:, j, c, :], in0=identf[:], scalar1=convwT[:, c, j:j + 1], scalar2=None,
                            op0=ALU.mult), R=[Bconst], W=[b])
                P.add("dve", lambda e: e.memset(st[:, 0, 3:4], 0.0), R=dbufs, W=[Bdiag])

            pt, Bpt = proj_fm(wgl, Bconst, 0, 16, xT, BxT, tk)
            P.add("act", lambda e, pt=pt: e.activation(out=glT[0:16, tk], in_=pt[0:16, tk], func=AF.Copy),
                  R=[Bpt], W=[BglT])
            if "wgv" in blk:
                wg, wv = blk["wgv"]
            else:
                wg = [wl(w_in, 512 + 256 * i) for i in range(2)]
                wv = [wl(w_in, 256 * i) for i in range(2)]
            def b2_chunk(c):
                pg, Bpg = proj_fm(wg[c // 2][0], wg[c // 2][1], (c % 2) * 128, 128, xT, BxT, tk)
                tsg, Btsg = gtmp()
                P.add("act", lambda e: e.activation(out=tsg[:, tk], in_=pg[:, tk], func=AF.Sigmoid), R=[Bpg], W=[Btsg])
                pv, Bpv = proj_fm(wv[c // 2][0], wv[c // 2][1], (c % 2) * 128, 128, xT, BxT, tk)
                P.add("dve", lambda e: e.tensor_tensor(out=vT[:, c, 32 + tk.start:32 + tk.stop], in0=pv[:, tk],
                                                       in1=tsg[:, tk], op=ALU.mult),
                      R=[Bpv, Btsg], W=[BvT[c]])

            zst = {}

            def z_tile(t):
                tsl = slice(t * 128, (t + 1) * 128)
                pz, Bpz = psum()
                P.add("pe", lambda e: e.matmul(pz[:, 0:256], lhsT=glT[0:32, tsl], rhs=wg2[:, :], start=True, stop=True),
                      R=[BglT, Bconst], W=[Bpz])
                te, Bte = gtmp()
                P.add("act", lambda e: e.activation(out=te[:, 0:256], in_=pz[:, 0:256], func=AF.Exp, scale=-1.0),
                      R=[Bpz], W=[Bte])
                tsp, Btsp = gtmp()
                P.add("act", lambda e: e.activation(out=tsp[:, 0:256], in_=te[:, 0:256], func=AF.Ln, bias=1.0),
                      R=[Bte], W=[Btsp])
                zst[t] = (tsp, Btsp)

            def decay_tile(t):
                tsl = slice(t * 128, (t + 1) * 128)
                tsp, Btsp = zst[t]
                pr, Bpr = psum()
                P.add("pe", lambda e: e.matmul(pr[:, 0:256], lhsT=urev[:], rhs=tsp[:, 0:256], start=True, stop=True),
                      R=[Btsp, Bconst], W=[Bpr])
                P.add("act", lambda e: e.activation(out=exprev[:, t, :], in_=pr[:, 0:256], func=AF.Exp),
                      R=[Bpr], W=[Bexprev[t]])
                pc, Bpc = psum()

                def mmc(e):
                    ins = None
                    for hp in range(2):
                        ins = e.matmul(pc[:, hp * 128:(hp + 1) * 128], lhsT=tsp[:, hp * 128:(hp + 1) * 128],
                                       rhs=ucum[:], start=True, stop=True)
                    return ins
                P.add("pe", mmc, R=[Btsp, Bconst], W=[Bpc])
                P.add("act", lambda e: e.activation(
                    out=Eb[:, :, tsl], in_=pc[:, 0:256].rearrange("p (a n) -> p a n", a=2), func=AF.Exp),
                    R=[Bpc], W=[BE[t]])
                if not preamble:
                    P.add("act", lambda e: e.activation(
                        out=Einv[:, :, tsl], in_=pc[:, 0:256].rearrange("p (a n) -> p a n", a=2), func=AF.Exp,
                        scale=-1.0), R=[Bpc], W=[BEi[t]])

            if preamble:
                z_tile(NT - 1)
                for c in range(4):
                    b2_chunk(c)
                decay_tile(NT - 1)
            else:
                b2_chunk(0)
                z_tile(0)
                b2_chunk(1)
                z_tile(1)
                decay_tile(0)
                b2_chunk(2)
                z_tile(2)
                decay_tile(1)
                b2_chunk(3)
                z_tile(3)
                decay_tile(2)

            wk, Bwk = blk["wk"] if "wk" in blk else wl(w_in, 1280)
            if not preamble:
                wq, Bwq = wl(w_in, 1024)
                qk = {}
                for hp in range(2):
                    qk[hp] = (proj_fm(wq, Bwq, hp * 128, 128, xT, BxT), proj_fm(wk, Bwk, hp * 128, 128, xT, BxT))
                    if hp == 0:
                        decay_tile(3)
                    (pq, Bpq), (pk, Bpk) = qk[hp]
                    P.add("dve", lambda e, pq=pq, hp=hp: e.scalar_tensor_tensor(
                        out=q_in[:, hp, :], in0=pq[:], scalar=0.125, in1=Eb[:, hp, :], op0=ALU.mult, op1=ALU.mult),
                        R=[Bpq] + BE, W=[Bq[hp]])
                    P.add("dve", lambda e, pk=pk, hp=hp: e.tensor_tensor(
                        out=k_in[:, hp, :], in0=pk[:], in1=Einv[:, hp, :], op=ALU.mult),
                        R=[Bpk] + BEi, W=[Bk[hp]])

            if not preamble:
                p1, Bp1 = psum()
                p2, Bp2 = psum()
                cst = {}

                def conv_ops(c):
                    pcv, Bpcv = psum()

                    def mmconv(e, pcv=pcv, c=c):
                        ins = None
                        for j in range(NTAP):
                            ins = e.matmul(pcv[:], lhsT=diag[:, j, c, :], rhs=vT[:, c, 2 + j:2 + j + T],
                                           start=(j == 0), stop=(j == NTAP - 1))
                        return ins
                    P.add("pe", mmconv, R=[BvT[c], Bhist, Bdiag], W=[Bpcv])
                    P.add("act", lambda e: e.activation(out=ycv[:, c, :], in_=pcv[:], func=AF.Identity,
                                                        bias=cvec[:, 0, c:c + 1]),
                          R=[Bpcv, Bconst], W=Bycv[c])
                    iq = nxt("ysq", 2)
                    P.add("act", lambda e: e.activation(out=ysq[iq][:].bitcast(BF16)[:, 0:T], in_=pcv[:], func=AF.Square,
                                                        bias=cvec[:, 0, c:c + 1]),
                          R=[Bpcv, Bconst], W=Bysq[iq])
                    cst[c] = iq

                def ones_ops(c):
                    iq = cst[c]
                    P.add("pe", lambda e: e.matmul(p1[:], lhsT=ones[:], rhs=ycv[:, c, :], start=(c == 0), stop=(c == 3)),
                          R=Bycv[c] + [Bconst], W=[Bp1])
                    P.add("pe", lambda e: e.matmul(p2[:], lhsT=ones_bf[:], rhs=ysq[iq][:].bitcast(BF16)[:, 0:T],
                                                   start=(c == 0), stop=(c == 3)),
                          R=Bysq[iq] + [Bconst], W=[Bp2])

                conv_ops(0)
                conv_ops(1)
                ones_ops(0)
                conv_ops(2)
                ones_ops(1)
                conv_ops(3)
                ones_ops(2)
                ones_ops(3)
                P.add("dve", lambda e: e.tensor_scalar(out=mu[:], in0=p1[:], scalar1=1.0 / CC, scalar2=None, op0=ALU.mult),
                      R=[Bp1], W=Bmu)
                P.add("act", lambda e: e.activation(out=rsd[:], in_=p1[:], func=AF.Square, scale=1.0 / CC),
                      R=[Bp1], W=Brsd)
                P.add("dve", lambda e: e.scalar_tensor_tensor(out=rsd[:], in0=p2[:], scalar=1.0 / CC, in1=rsd[:],
                                                              op0=ALU.mult, op1=ALU.subtract), R=[Bp2] + Brsd, W=Brsd)

                def conv_normalize():
                    P.add("act", lambda e: e.activation(out=rsd[:], in_=rsd[:], func=AF.Ln, bias=LN_EPS), R=Brsd, W=Brsd)
                    P.add("act", lambda e: e.activation(out=rsd[:], in_=rsd[:], func=AF.Exp, scale=-0.5), R=Brsd, W=Brsd)
                    for c in range(4):
                        t1, Bt1 = gtmp()
                        P.add("dve", lambda e, t1=t1, c=c: e.tensor_tensor(out=t1[:], in0=ycv[:, c, :], in1=mu[:],
                                                                           op=ALU.subtract), R=Bycv[c] + Bmu, W=[Bt1])
                        P.add("dve", lambda e, t1=t1: e.tensor_tensor(out=t1[:], in0=t1[:], in1=rsd[:], op=ALU.mult),
                              R=[Bt1] + Brsd, W=[Bt1])
                        P.add("act", lambda e, t1=t1, c=c: e.activation(out=yT[:, c, :], in_=t1[:], func=AF.Silu,
                                                                        bias=cvec[:, 2, c:c + 1], scale=cvec[:, 1, c:c + 1]),
                              R=[Bt1, Bconst], W=[ByTc[c]])
                conv_normalize()

            wvt = [blk["wvt0"] if "wvt0" in blk else wl(w_in, 1536), wl(w_in, 1792)]
            for t in tiles:
                tsl = slice(t * 128, (t + 1) * 128)
                pv, Bpv = psum()

                def mmv(e, pv=pv, tsl=tsl):
                    ins = None
                    for hf in range(2):
                        for kc in range(KC):
                            ins = e.matmul(pv[:, hf * 256:(hf + 1) * 256], lhsT=xT[:, kc, tsl], rhs=wvt[hf][0][:, kc, :],
                                           start=(kc == 0), stop=(kc == KC - 1))
                    return ins
                P.add("pe", mmv, R=[wvt[0][1], wvt[1][1], BxT[t]], W=[Bpv])
                P.add("act", lambda e, pv=pv, t=t: e.activation(out=v_tm[:, t, :], in_=pv[:], func=AF.Copy),
                      R=[Bpv], W=[Bvtm[t]])
                pk, Bpk = psum()

                def mmk(e, pk=pk, tsl=tsl):
                    ins = None
                    for kc in range(KC):
                        ins = e.matmul(pk[:, 0:256], lhsT=xT[:, kc, tsl], rhs=wk[:, kc, :], start=(kc == 0),
                                       stop=(kc == KC - 1))
                    return ins
                P.add("pe", mmk, R=[Bwk, BxT[t]], W=[Bpk])
                ko = k_out[:, t, :].rearrange("p (a b n) -> p a b n", a=2, b=2)
                pk4 = pk[:, 0:256].rearrange("p (a b n) -> p a b n", a=2, b=2)
                er4 = exprev[:, t, :].rearrange("p (a b n) -> p a b n", a=2, b=2)
                for hh in range(2):
                    P.add("dve", lambda e, ko=ko, pk4=pk4, er4=er4, hh=hh: e.tensor_tensor(
                        out=ko[:, :, hh, hh * 64:hh * 64 + 64], in0=pk4[:, :, hh, :], in1=er4[:, :, hh, :],
                        op=ALU.mult), R=[Bpk, Bexprev[t]], W=[Bkout[t]])

            if not preamble:
                wgg = [wl(w_in, 2048 + 256 * i) for i in range(2)]
                for hh in range(4):
                    pg, Bpg = proj_fm(wgg[hh // 2][0], wgg[hh // 2][1], (hh % 2) * 128, 128, xT, BxT)
                    P.add("act", lambda e, pg=pg, hh=hh: e.activation(out=sgT[:, hh, :], in_=pg[:], func=AF.Silu),
                          R=[Bpg], W=[BsgT[hh]])

            scs = {}
            def d2_scores(t):
                tsl = slice(t * 128, (t + 1) * 128)
                pscs = [psum(), psum()]

                def mmsc(e, pscs=pscs, tsl=tsl):
                    ins = None
                    for hh in range(2):
                        for hp in range(2):
                            ins = e.matmul(pscs[hh][0][:, hp * 128:(hp + 1) * 128],
                                           lhsT=k_in[hh * 64:(hh + 1) * 64, hp, tsl],
                                           rhs=q_in[hh * 64:(hh + 1) * 64, hp, tsl], start=True, stop=True)
                    return ins
                P.add("pe", mmsc, R=Bq + Bk, W=[pscs[0][1], pscs[1][1]])
                isc = nxt("sc", 2)
                scv = sc_bf[isc][:].rearrange("p (a b) n -> p a b n", b=2)
                for hh in range(2):
                    P.add("dve", lambda e, pscs=pscs, scv=scv, hh=hh: e.tensor_tensor(
                        out=scv[:, :, hh, :], in0=pscs[hh][0][:, 0:256].rearrange("p (a n) -> p a n", a=2),
                        in1=mask[:].unsqueeze(1).to_broadcast([128, 2, 128]), op=ALU.mult),
                        R=[pscs[hh][1], Bconst], W=[Bsc[isc]])
                return isc

            if not preamble:
                scs[0] = d2_scores(0)
                scs[1] = d2_scores(1)
            kvb = [psum() for _ in range(4)]

            def kv_ap(t, n, hp):
                return kvb[n * 2 + t // 2][0][:, (t % 2) * 256 + hp * 128:(t % 2) * 256 + (hp + 1) * 128]

            for n in range(2):
                def mmkv(e, n=n):
                    ins = None
                    rows = slice(n * 64, (n + 1) * 64)
                    for t in tiles:
                        for hp in range(2):
                            for hh in range(2):
                                hd = 2 * hp + hh
                                ins = e.matmul(kv_ap(t, n, hp), lhsT=k_out[rows, t, hd * 128:(hd + 1) * 128],
                                               rhs=v_tm[rows, t, hd * 128:(hd + 1) * 128], start=(hh == 0), stop=(hh == 1))
                    return ins
                P.add("pe", mmkv, R=Bkout + Bvtm, W=[kvb[n * 2][1], kvb[n * 2 + 1][1]])
            for m in range(2 * tiles[0], 2 * NT):
                t, n = divmod(m, 2)
                if not preamble:
                    P.add("dve", lambda e, m=m: e.tensor_copy(out=S_snap[:, m, :, :], in_=S[:]),
                          R=[BS], W=[Bsnap[m]])
                lastc = t * 128 + n * 64 + 63
                for hp in range(2):
                    P.add("dve", lambda e, t=t, n=n, hp=hp, lastc=lastc: e.scalar_tensor_tensor(
                        out=S[:, hp, :], in0=S[:, hp, :], scalar=Eb[:, hp, lastc:lastc + 1],
                        in1=kv_ap(t, n, hp), op0=ALU.mult, op1=ALU.add),
                        R=[BS, BE[t], kvb[n * 2 + t // 2][1]], W=[BS])

            if not preamble:
                wo = [wl(w_out, 256 * i) for i in range(4)]

                def wout_tile(t):
                    tsl = slice(t * 128, (t + 1) * 128)
                    for cb in range(2):
                        pt, Bpt = psum()

                        def mmo2(e, pt=pt, tsl=tsl, cb=cb):
                            ins = None
                            for hf in range(2):
                                for kc in range(KC):
                                    ins = e.matmul(pt[:, hf * 256:(hf + 1) * 256], lhsT=yT[:, kc, tsl],
                                                   rhs=wo[2 * cb + hf][0][:, kc, :], start=(kc == 0), stop=(kc == KC - 1))
                            return ins
                        P.add("pe", mmo2, R=[wo[2 * cb][1], wo[2 * cb + 1][1], ByTg[t]] + ByTc, W=[Bpt])
                        P.add("dve", lambda e, pt=pt, t=t, cb=cb: e.tensor_tensor(
                            out=h[hs][t][:, cb * 512:(cb + 1) * 512], in0=h[hs][t][:, cb * 512:(cb + 1) * 512], in1=pt[:],
                            op=ALU.add), R=[Bpt, Bh[hs][t]], W=[Bh[hs][t]])


                def d2_out(t, isc):
                    tsl = slice(t * 128, (t + 1) * 128)
                    po, Bpo = psum()
                    for n in range(2):
                        m = 2 * t + n

                        def mmo(e, po=po, isc=isc, n=n, t=t, m=m):
                            ins = None
                            for hd in range(4):
                                hp, hh = hd // 2, hd % 2
                                oc = slice(hd * 128 + n * 64, hd * 128 + n * 64 + 64)
                                e.matmul(po[:, oc], lhsT=v_tm[:, t, hd * 128:(hd + 1) * 128],
                                         rhs=sc_bf[isc][:, hd, n * 64:(n + 1) * 64], start=True, stop=False)
                                ins = e.matmul(po[:, oc], lhsT=S_snap[hh * 64:(hh + 1) * 64, m, hp, :],
                                               rhs=q_in[hh * 64:(hh + 1) * 64, hp, t * 128 + n * 64: t * 128 + n * 64 + 64],
                                               start=False, stop=True)
                            return ins
                        P.add("pe", mmo, R=[Bvtm[t], Bsc[isc], Bsnap[m]] + Bq, W=[Bpo])
                    return po, Bpo

                def d2_norm(t, po, Bpo):
                    tsl = slice(t * 128, (t + 1) * 128)
                    tq, Btq = gtmp()
                    P.add("act", lambda e, po=po, tq=tq: e.activation(out=tq[:].bitcast(BF16)[:, 0:T], in_=po[:], func=AF.Square),
                          R=[Bpo], W=[Btq])
                    pss, Bpss = psum()
                    P.add("pe", lambda e, pss=pss, tq=tq: e.matmul(pss[:], lhsT=ones_bf[:], rhs=tq[:].bitcast(BF16)[:, 0:T],
                                                                   start=True, stop=True),
                          R=[Btq, Bconst], W=[Bpss])
                    tr_, Btr = gtmp()
                    P.add("act", lambda e, pss=pss, tr_=tr_: e.activation(out=tr_[:], in_=pss[:], func=AF.Ln,
                                                                          bias=RMS_EPS, scale=1.0 / 128),
                          R=[Bpss], W=[Btr])
                    P.add("act", lambda e, tr_=tr_: e.activation(out=tr_[:], in_=tr_[:], func=AF.Exp, scale=-0.5),
                          R=[Btr], W=[Btr])
                    P.add("dve", lambda e, po=po, tr_=tr_: e.tensor_tensor(out=tr_[:], in0=po[:], in1=tr_[:], op=ALU.mult),
                          R=[Bpo, Btr], W=[Btr])
                    P.add("dve", lambda e, tr_=tr_, tsl=tsl: e.scalar_tensor_tensor(
                        out=yT[:, 4:8, tsl], in0=tr_[:].rearrange("p (a n) -> p a n", a=4), scalar=gng[:, 0:1],
                        in1=sgT[:, :, tsl], op0=ALU.mult, op1=ALU.mult),
                        R=[Btr, Bconst] + BsgT, W=[ByTg[t]])


                def nt(t):
                    norm_transpose(hs, t, gffn, xT, BxT[t])

                o0 = d2_out(0, scs[0])
                scs[2] = d2_scores(2)
                o1 = d2_out(1, scs[1])
                d2_norm(0, *o0)
                scs[3] = d2_scores(3)
                o2 = d2_out(2, scs[2])
                d2_norm(1, *o1)
                wout_tile(0)
                o3 = d2_out(3, scs[3])
                d2_norm(2, *o2)
                wout_tile(1)
                nt(0)
                d2_norm(3, *o3)
                wout_tile(2)
                nt(1)
                wout_tile(3)
                nt(2)
                nt(3)

            P.add("pool", lambda e: e.tensor_copy(out=vT[:, :, 0:32], in_=vT[:, :, T:T + 32]), R=BvT, W=[Bhist])
            if preamble:
                P.add("pool", lambda e: e.tensor_copy(out=S0[:], in_=S[:]), R=[BS], W=[BS0])
                P.add("pool", lambda e: e.tensor_copy(out=hist0[:], in_=vT[:, :, 0:32]), R=[Bhist], W=[Bhist0])
                if nxt_blk is not None:
                    front(nxt_blk)
                    front_tr(nxt_blk)
                return

            if nxt_blk is not None:
                front(nxt_blk)
            for j0 in range(0, NJ, 2):
                wgv, Bwg_ = wl(w_fg, j0 * 128)
                wuv, Bwu_ = wl(w_fu, j0 * 128)
                for jj in range(2):
                    j = j0 + jj
                    pg, Bpg = proj_fm(wgv, Bwg_, jj * 128, 128, xT, BxT)
                    pu, Bpu = proj_fm(wuv, Bwu_, jj * 128, 128, xT, BxT)
                    ts_, Bts = gtmp()
                    P.add("act", lambda e, pg=pg, ts_=ts_: e.activation(out=ts_[:], in_=pg[:], func=AF.Silu),
                          R=[Bpg], W=[Bts])
                    P.add("dve", lambda e, pu=pu, ts_=ts_, j=j: e.tensor_tensor(out=actT[:, j, :], in0=pu[:], in1=ts_[:],
                                                                               op=ALU.mult), R=[Bpu, Bts], W=[Bact[j]])
            if nxt_blk is not None:
                front_tr(nxt_blk)
            for cb in range(2):
                acc = [psum() for _ in range(NT)]
                for j0 in range(0, NJ, 2):
                    wd_, Bwd = wslot()
                    wd = wd_[:, 0:1024].rearrange("p (j n) -> p j n", j=2)
                    wload(wd, w_fd[j0 * 128:(j0 + 2) * 128, cb * 512:(cb + 1) * 512].rearrange("(j p) n -> p j n", p=128), Bwd)

                    def mmd(e, wd=wd, j0=j0, acc=acc):
                        ins = None
                        for jj in range(2):
                            j = j0 + jj
                            for t in range(NT):
                                ins = e.matmul(acc[t][0][:], lhsT=actT[:, j, t * 128:(t + 1) * 128], rhs=wd[:, jj, :],
                                               start=(j == 0), stop=(j == NJ - 1))
                        return ins
                    P.add("pe", mmd, R=[Bwd] + Bact[j0:j0 + 2], W=[a[1] for a in acc])
                    if cb == 1 and j0 == NJ - 2 and nxt_blk is not None:
                        nxt_blk["wgv"] = ([wl(w_in, 512 + 256 * i) for i in range(2)],
                                          [wl(w_in, 256 * i) for i in range(2)])
                for t in range(NT):
                    P.add("dve", lambda e, t=t, cb=cb, a=acc[t][0]: e.tensor_tensor(
                        out=h[hs][t][:, cb * 512:(cb + 1) * 512], in0=h[hs][t][:, cb * 512:(cb + 1) * 512], in1=a[:],
                        op=ALU.add), R=[acc[t][1], Bh[hs][t]], W=[Bh[hs][t]])
            for t in range(NT):
                tj, Btj = gtmp()
                ht, Bht = h[hs][t], Bh[hs][t]
                P.add("act", lambda e, t=t, tj=tj, ht=ht: e.activation(out=tj[:].bitcast(BF16), in_=ht[:], func=AF.Square,
                                                                       accum_out=st[:, t, 4:5]),
                      R=[Bht], W=[Btj, Bst2[t]])
                P.add("act", lambda e, t=t: e.activation(out=st[:, t, 5:6], in_=st[:, t, 4:5], func=AF.Ln,
                                                         bias=RMS_EPS, scale=1.0 / D), R=[Bst2[t]], W=[Bst2[t]])
                P.add("act", lambda e, t=t: e.activation(out=st[:, t, 6:7], in_=st[:, t, 5:6], func=AF.Exp,
                                                         scale=-0.5), R=[Bst2[t]], W=[Bst2[t]])
                P.add("dve", lambda e, t=t, ht=ht: e.scalar_tensor_tensor(out=ht[:], in0=ht[:], scalar=st[:, t, 6:7],
                                                                          in1=gfin[:], op0=ALU.mult, op1=ALU.mult),
                      R=[Bht, Bst2[t], Bconst], W=[Bht])
                P.add("sp", lambda e, t=t, ht=ht: e.dma_start(out=out[out_row0 + t * 128: out_row0 + (t + 1) * 128, :],
                                                              in_=ht[:]),
                      R=[Bht], dma=True, dkey=Bht.name)

        blocks = [dict(hs=0, row0=0, orow0=0, first=False, pre=True, **pre_w)]
        for s_i in range(nseq):
            for b in range(nblk_seq):
                r = s_i * seqlen + b * T
                blocks.append(dict(hs=len(blocks) % 2, row0=T + r, orow0=r, first=(b == 0), pre=False))
        front(blocks[0])
        front_tr(blocks[0])
        P.add("dve", lambda e: e.memset(st[:, 0, 7:8], 0.0), R=list(setup_bufs), W=[Bconst])
        for bi, blk in enumerate(blocks):
            rest(blk, blocks[bi + 1] if bi + 1 < len(blocks) else None)
        P.add("sp", None, W=Bh[0] + Bh[1])
        P.finalize()
        dkeys = sorted(P.dcount.keys())
        esem = {e: es.enter_context(nc.semaphore(f"sem_{e}")) for e in Prog.ENGS}
        dsem = {k: es.enter_context(nc.semaphore(f"dsem_{k}")) for k in dkeys}
        with nc.Block() as blk:
            @blk.tensor
            def _(e):
                P.emit("pe", e, esem, dsem)

            @blk.scalar
            def _(e):
                P.emit("act", e, esem, dsem)

            @blk.vector
            def _(e):
                P.emit("dve", e, esem, dsem)

            @blk.gpsimd
            def _(e):
                P.emit("pool", e, esem, dsem)

            @blk.sync
            def _(e):
                P.emit("sp", e, esem, dsem)
    return nc


def make_in_maps(inputs, nseq, seqlen, n_cores):
    x = np.ascontiguousarray(inputs["x"], dtype=np.float32)
    meta = np.asarray(inputs["meta_tokens"], dtype=np.float32)
    pre = np.zeros((T, D), np.float32)
    pre[T - meta.shape[0]:] = meta
    conv_wT = np.ascontiguousarray(
        np.asarray(inputs["conv_w"][0], np.float32).T.reshape(4, 128, NTAP).transpose(1, 0, 2))
    vecs = np.stack([np.asarray(inputs[k][0], np.float32).reshape(4, 128).T
                     for k in ("conv_b", "conv_ln_g", "conv_ln_b")], axis=1)
    common = {
        "w_in": np.ascontiguousarray(inputs["w_in"][0], dtype=np.float32),
        "w_out": np.ascontiguousarray(inputs["w_out"][0], dtype=np.float32),
        "w_ffn_gate": np.ascontiguousarray(inputs["w_ffn_gate"][0], dtype=np.float32),
        "w_ffn_up": np.ascontiguousarray(inputs["w_ffn_up"][0], dtype=np.float32),
        "w_ffn_down": np.ascontiguousarray(inputs["w_ffn_down"][0], dtype=np.float32),
        "norm_mix_g": np.asarray(inputs["norm_mix_g"], np.float32).reshape(1, D),
        "norm_ffn_g": np.asarray(inputs["norm_ffn_g"], np.float32).reshape(1, D),
        "norm_final_g": np.asarray(inputs["norm_final_g"], np.float32).reshape(1, D),
        "conv_wT": conv_wT,
        "conv_vecs": np.ascontiguousarray(vecs),
        "gla_w_gate2": np.ascontiguousarray(inputs["gla_w_gate2"][0], dtype=np.float32),
        "gla_gate_b": np.asarray(inputs["gla_gate_b"], np.float32).reshape(1, 256),
        "gla_norm_g": np.asarray(inputs["gla_norm_g"], np.float32).reshape(128, 1),
    }
    maps = []
    for c in range(n_cores):
        xc = x[c * nseq:(c + 1) * nseq].reshape(nseq * seqlen, D)
        m = dict(common)
        m["xin"] = np.concatenate([pre, xc], axis=0)
        maps.append(m)
    return maps


def kernel(**inputs):
    x = inputs["x"]
    bsz, seqlen, _ = x.shape
    nseq = bsz // N_CORES
    nc = build_program(nseq, seqlen)
    in_maps = make_in_maps(inputs, nseq, seqlen, N_CORES)
    res = run_bass_kernel_spmd(nc, in_maps, core_ids=list(range(N_CORES)))
    outs = [np.asarray(r["out"], dtype=np.float32).reshape(nseq, seqlen, D) for r in res.results]
    return np.concatenate(outs, axis=0)
```

```python
import contextlib
import numpy as np
import concourse.bass as bass
import concourse.mybir as mybir
from concourse.bass_utils import run_bass_kernel_spmd

F32 = mybir.dt.float32
BF16 = mybir.dt.bfloat16
AF = mybir.ActivationFunctionType
ALU = mybir.AluOpType

D = 1024
KC = 8
DIN = 2576
CC = 512
DFF = 2816
NJ = 22
T = 512
NT = 4
NTAP = 31
N_CORES = 8
RMS_EPS = 1e-6
LN_EPS = 1e-5


class Buf:
    def __init__(self, name, psum=False):
        self.name = name
        self.lw = None
        self.rd = []
        self.psum = psum


class Op:
    __slots__ = ("eng", "fn", "deps", "sig", "ticket", "dma", "dkey", "dval", "gi")

    def __init__(self, eng, fn, dma, dkey):
        self.eng = eng
        self.fn = fn
        self.deps = []
        self.sig = False
        self.ticket = 0
        self.dma = dma
        self.dkey = dkey
        self.dval = 0
        self.gi = 0


class Prog:
    ENGS = ("pe", "act", "dve", "pool", "sp")

    def __init__(self):
        self.ops = {e: [] for e in self.ENGS}
        self.n = 0
        self.dcount = {}

    def add(self, eng, fn, R=(), W=(), dma=False, dkey=None):
        if dma and dkey is None:
            dkey = W[0].name if W else R[0].name
        op = Op(eng, fn, dma, dkey)
        op.gi = self.n
        self.n += 1
        deps = {}
        for b in R:
            if b.lw is not None:
                deps[id(b.lw)] = b.lw
            if b.psum:
                for r in b.rd:
                    if r.eng != eng:
                        deps[id(r)] = r
        for b in W:
            if b.lw is not None:
                deps[id(b.lw)] = b.lw
            for r in b.rd:
                deps[id(r)] = r
        deps.pop(id(op), None)
        op.deps = list(deps.values())
        for b in R:
            b.rd.append(op)
        for b in W:
            b.lw = op
            b.rd = []
        if dma:
            self.dcount[dkey] = self.dcount.get(dkey, 0) + 16
            op.dval = self.dcount[dkey]
        self.ops[eng].append(op)
        return op

    @staticmethod
    def _needs_wait(a, b):
        if a.dma:
            return True
        if a.eng == "pe" and b.eng == "pe":
            return False
        return True

    def finalize(self):
        for e in self.ENGS:
            for b in self.ops[e]:
                for a in b.deps:
                    if not a.dma and self._needs_wait(a, b):
                        a.sig = True
        for e in self.ENGS:
            c = 0
            for a in self.ops[e]:
                if a.sig and not a.dma:
                    c += 1
                    a.ticket = c

    def emit(self, eng_name, eng, esem, dsem):
        seen = {}
        for b in self.ops[eng_name]:
            waits = {}
            for a in b.deps:
                if not self._needs_wait(a, b):
                    continue
                if a.dma:
                    s, v = dsem[a.dkey], (self.dcount[a.dkey] if a.dkey == "const" else a.dval)
                else:
                    s, v = esem[a.eng], a.ticket
                k = id(s)
                if k not in waits or waits[k][1] < v:
                    waits[k] = (s, v)
            for k, (s, v) in waits.items():
                if seen.get(k, 0) >= v:
                    continue
                seen[k] = v
                eng.wait_ge(s, v)
            if b.fn is None:
                continue
            ins = b.fn(eng)
            if b.dma:
                ins.then_inc(dsem[b.dkey], 16)
            elif b.sig:
                ins.then_inc(esem[eng_name], 1)


def build_program(nseq, seqlen):
    assert seqlen % T == 0
    nblk_seq = seqlen // T
    ntok = nseq * seqlen
    nc = bass.Bass("TRN2", target_bir_lowering=False)
    P = Prog()

    def din(name, shape, dt=F32):
        return nc.dram_tensor(name, list(shape), dt, kind="ExternalInput").ap()

    xin = din("xin", [T + ntok, D])
    w_in = din("w_in", [D, DIN])
    w_out = din("w_out", [D, D])
    w_fg = din("w_ffn_gate", [D, DFF])
    w_fu = din("w_ffn_up", [D, DFF])
    w_fd = din("w_ffn_down", [DFF, D])
    g_mix_d = din("norm_mix_g", [1, D])
    g_ffn_d = din("norm_ffn_g", [1, D])
    g_fin_d = din("norm_final_g", [1, D])
    convwT_d = din("conv_wT", [128, 4, NTAP])
    cvec_d = din("conv_vecs", [128, 3, 4])
    wg2_d = din("gla_w_gate2", [16, 256])
    gb_d = din("gla_gate_b", [1, 256])
    gng_d = din("gla_norm_g", [128, 1])
    out = nc.dram_tensor("out", [ntok, D], F32, kind="ExternalOutput").ap()

    es = contextlib.ExitStack()
    with es:
        def sb(name, shape, dt=F32):
            return es.enter_context(nc.sbuf_tensor(name, list(shape), dt))

        h = [[sb(f"h{s_}_{t}", [128, D]) for t in range(NT)] for s_ in range(2)]
        st = sb("st", [128, NT, 8])
        NXN = 4
        xn_tm = [sb(f"xn_tm{i}", [128, D], BF16) for i in range(NXN)]
        xT = sb("xT", [128, KC, T], BF16)
        yT = sb("yT", [128, KC, T], BF16)
        vT = sb("vT", [128, 4, 32 + T], BF16)
        hist0 = sb("hist0", [128, 4, 32], BF16)
        NTMP = 5
        tmp = [sb(f"tmp{i}", [128, T]) for i in range(NTMP)]
        glT = sb("glT", [32, T])
        exprev = sb("exprev", [128, NT, 256])
        Eb = sb("Eb", [128, 2, T])
        Einv = sb("Einv", [128, 2, T])
        q_in = sb("q_in", [128, 2, T], BF16)
        k_in = sb("k_in", [128, 2, T], BF16)
        v_tm = sb("v_tm", [128, NT, 512], BF16)
        k_out = sb("k_out", [128, NT, 512], BF16)
        sgT = sb("sgT", [128, 4, T])
        sc_bf = [sb(f"sc_bf{i}", [128, 4, 128], BF16) for i in range(2)]
        S = sb("S", [128, 2, 128])
        S0 = sb("S0", [128, 2, 128])
        S_snap = sb("S_snap", [128, 2 * NT, 2, 128], BF16)
        actT = sb("actT", [128, NJ, T], BF16)

        def act_f32(j0, nch):
            return actT[:, j0:j0 + nch, :].rearrange("p a n -> p (a n)").bitcast(F32)
        ycv = act_f32(14, 8).rearrange("p (c n) -> p c n", c=4)
        ysq = [act_f32(10 + 2 * i, 2) for i in range(2)]
        mu = act_f32(8, 2)
        rsd = act_f32(6, 2)
        NW = 6
        wring = [sb(f"wring{i}", [128, 2048], BF16) for i in range(NW)]
        gmix = sb("gmix", [128, D])
        gffn = sb("gffn", [128, D])
        gfin = sb("gfin", [128, D])
        diag = sb("diag", [128, NTAP, 4, 128], BF16)
        identf = sb("identf", [128, 128])
        ident = sb("ident", [128, 128], BF16)
        ones = sb("ones", [128, 128])
        ones_bf = sb("ones_bf", [128, 128], BF16)
        urev = sb("urev", [128, 128])
        ucum = sb("ucum", [128, 128])
        mask = sb("mask", [128, 128])
        wg2 = sb("wg2", [32, 256])
        wgl = sb("wgl", [128, KC, 16], BF16)
        convwT = sb("convwT", [128, 4, NTAP])
        cvec = sb("cvec", [128, 3, 4])
        gng = sb("gng", [128, 1])

        NPS = 8
        ps = [es.enter_context(nc.psum_tensor(f"ps{i}", [128, 512], F32)) for i in range(NPS)]

        Bh = [[Buf(f"h{s_}_{t}") for t in range(NT)] for s_ in range(2)]
        Bst = [Buf(f"st{t}") for t in range(NT)]
        Bst2 = [Buf(f"stf{t}") for t in range(NT)]
        Bxn = [Buf(f"xn{i}") for i in range(NXN)]
        BxT = [Buf(f"xT{t}") for t in range(NT)]
        ByTc = [Buf(f"yTc{c}") for c in range(4)]
        ByTg = [Buf(f"yTg{t}") for t in range(NT)]
        BvT = [Buf(f"vT{c}") for c in range(4)]
        Bhist = Buf("hist")
        Bhist0 = Buf("hist0")
        Btmp = [Buf(f"tmp{i}") for i in range(NTMP)]
        BglT = Buf("glT")
        Bexprev = [Buf(f"exprev{t}") for t in range(NT)]
        BE = [Buf(f"E{t}") for t in range(NT)]
        BEi = [Buf(f"Ei{t}") for t in range(NT)]
        Bq = [Buf(f"q{hp}") for hp in range(2)]
        Bk = [Buf(f"k{hp}") for hp in range(2)]
        Bvtm = [Buf(f"vtm{t}") for t in range(NT)]
        Bkout = [Buf(f"kout{t}") for t in range(NT)]
        BsgT = [Buf(f"sgT{hh}") for hh in range(4)]
        Bsc = [Buf(f"sc{i}") for i in range(2)]
        BS = Buf("S")
        BS0 = Buf("S0")
        Bsnap = [Buf(f"snap{m}") for m in range(2 * NT)]
        Bdiag = Buf("diag")
        Bact = [Buf(f"act{j}") for j in range(NJ)]
        Bycv = [[Bact[14 + 2 * c], Bact[15 + 2 * c]] for c in range(4)]
        Bysq = [[Bact[10 + 2 * i], Bact[11 + 2 * i]] for i in range(2)]
        Bmu = [Bact[8], Bact[9]]
        Brsd = [Bact[6], Bact[7]]
        Bw = [Buf(f"w{i}") for i in range(NW)]
        Bps = [Buf(f"ps{i}", psum=True) for i in range(NPS)]
        Bconst = Buf("const")

        rr = {"ps": 0, "tmp": 0, "w": 0, "xn": 0, "ysq": 0, "sc": 0}

        def nxt(kind, n):
            i = rr[kind]
            rr[kind] = (i + 1) % n
            return i

        def psum():
            i = nxt("ps", NPS)
            return ps[i], Bps[i]

        def gtmp():
            i = nxt("tmp", NTMP)
            return tmp[i], Btmp[i]

        def wslot():
            i = nxt("w", NW)
            return wring[i], Bw[i]

        setup_bufs = []

        def sbuf_():
            b = Buf(f"setup{len(setup_bufs)}")
            setup_bufs.append(b)
            return b

        def cload(dst_ap, src_ap, eng="sp", after=(), dkey=None):
            b = sbuf_()
            P.add(eng, lambda e: e.dma_start(out=dst_ap, in_=src_ap), R=list(after), W=[b], dma=True,
                  dkey=dkey or ("const" if eng == "sp" else "constp"))
            return b

        Bid = sbuf_()
        P.add("pool", lambda e: e.memset(identf[:], 0.0), W=[Bid])
        P.add("pool", lambda e: e.affine_select(out=identf[:], in_=identf[:], pattern=[[-1, 128]],
                                                compare_op=ALU.not_equal, fill=1.0, base=0,
                                                channel_multiplier=1), R=[Bid], W=[Bid])
        cload(wgl[:], w_in[:, 2560:2576].rearrange("(kc p) n -> p kc n", p=128), eng="pool")
        pre_row = (NT - 1) * 128
        P.add("sp", lambda e: e.dma_start(out=h[0][NT - 1][:], in_=xin[pre_row:pre_row + 128, :]),
              W=[Bh[0][NT - 1]], dma=True)

        def early_wl(c0):
            s_, B_ = wslot()
            v = s_[:, 0:KC * 256].rearrange("p (k n) -> p k n", k=KC)
            P.add("pool", lambda e: e.dma_start(out=v, in_=w_in[:, c0:c0 + 256].rearrange("(kc p) n -> p kc n", p=128)),
                  W=[B_], dma=True)
            return v, B_
        pre_w = dict(wgv=([early_wl(512), early_wl(768)], [early_wl(0), early_wl(256)]),
                     wk=early_wl(1280), wvt0=early_wl(1536))


        Bgain = {id(gmix): cload(gmix[:], g_mix_d.partition_broadcast(128), dkey="c_gmix")}
        Bident = sbuf_()
        P.add("dve", lambda e: e.tensor_copy(out=ident[:], in_=identf[:]), R=[Bid], W=[Bident])
        Bwg2m = sbuf_()
        P.add("dve", lambda e: e.memset(wg2[:], 0.0), W=[Bwg2m])
        BconvwT = cload(convwT[:], convwT_d)
        cload(cvec[:], cvec_d)
        cload(gng[:], gng_d)
        cload(wg2[0:16, :], wg2_d, after=[Bwg2m])
        cload(wg2[16:17, :], gb_d, after=[Bwg2m])

        def tri(dst, val, keep_le):
            b = sbuf_()
            P.add("pool", lambda e: e.memset(dst[:], val), W=[b])
            if keep_le:
                P.add("pool", lambda e: e.affine_select(out=dst[:], in_=dst[:], pattern=[[1, 128]],
                                                        compare_op=ALU.is_ge, fill=0.0, base=0,
                                                        channel_multiplier=-1), R=[b], W=[b])
            else:
                P.add("pool", lambda e: e.affine_select(out=dst[:], in_=dst[:], pattern=[[-1, 128]],
                                                        compare_op=ALU.is_ge, fill=0.0, base=-1,
                                                        channel_multiplier=1), R=[b], W=[b])
            P.add("pool", lambda e: e.memset(dst[0:64, 64:128], 0.0), R=[b], W=[b])
            P.add("pool", lambda e: e.memset(dst[64:128, 0:64], 0.0), R=[b], W=[b])

        tri(ucum, -1.0 / 16, True)
        tri(urev, -1.0 / 16, False)
        tri(mask, 1.0, True)
        Bgain[id(gffn)] = cload(gffn[:], g_ffn_d.partition_broadcast(128))
        Bgain[id(gfin)] = cload(gfin[:], g_fin_d.partition_broadcast(128))
        P.add("dve", lambda e: e.memset(ones[:], 1.0), W=[sbuf_()])
        P.add("dve", lambda e: e.memset(ones_bf[:], 1.0), W=[sbuf_()])
        P.add("dve", lambda e: e.memset(glT[:], 1.0), W=[BglT])
        P.add("dve", lambda e: e.memset(k_out[:], 0.0), W=Bkout)
        P.add("dve", lambda e: e.memset(S[:], 0.0), W=[BS])
        P.add("dve", lambda e: e.memset(vT[:, :, 0:32], 0.0), W=[Bhist])

        def norm_part(hs, t, gb):
            i = nxt("xn", NXN)
            ht, Bht = h[hs][t], Bh[hs][t]
            P.add("act", lambda e: e.activation(out=xn_tm[i][:], in_=ht[:], func=AF.Square,
                                                accum_out=st[:, t, 0:1]), R=[Bht], W=[Bxn[i], Bst[t]])
            P.add("act", lambda e: e.activation(out=st[:, t, 1:2], in_=st[:, t, 0:1], func=AF.Ln,
                                                bias=RMS_EPS, scale=1.0 / D), R=[Bst[t]], W=[Bst[t]])
            P.add("act", lambda e: e.activation(out=st[:, t, 2:3], in_=st[:, t, 1:2], func=AF.Exp,
                                                scale=-0.5), R=[Bst[t]], W=[Bst[t]])
            P.add("dve", lambda e: e.scalar_tensor_tensor(out=xn_tm[i][:], in0=ht[:], scalar=st[:, t, 2:3],
                                                          in1=gb[:], op0=ALU.mult, op1=ALU.mult),
                  R=[Bht, Bst[t], Bgain[id(gb)]], W=[Bxn[i]])
            return i

        def tr_part(i, t, dstT, BdstT):
            pt, Bpt = psum()
            ptb = pt[:].bitcast(BF16).rearrange("p (k n) -> p k n", k=KC)

            def tr(e):
                ins = None
                for kc in range(KC):
                    ins = e.transpose(out=ptb[:, kc, :], in_=xn_tm[i][:, kc * 128:(kc + 1) * 128], identity=ident[:])
                return ins
            P.add("pe", tr, R=[Bxn[i], Bident], W=[Bpt])
            P.add("act", lambda e: e.activation(out=dstT[:, :, t * 128:(t + 1) * 128], in_=ptb, func=AF.Copy),
                  R=[Bpt], W=[BdstT])

        def norm_transpose(hs, t, gb, dstT, BdstT):
            tr_part(norm_part(hs, t, gb), t, dstT, BdstT)

        def wload(dst_ap, src_ap, Bslot):
            P.add("pool", lambda e: e.dma_start(out=dst_ap, in_=src_ap), W=[Bslot], dma=True)

        def w_cols(w, c0, ncols):
            return w[:, c0:c0 + ncols].rearrange("(kc p) n -> p kc n", p=128)

        def proj_fm(wv, Bwv, col0, ncols, srcT, BsrcT, tk=slice(0, T)):
            pt, Bpt = psum()

            def mm(e):
                ins = None
                for kc in range(KC):
                    ins = e.matmul(pt[0:ncols, tk], lhsT=wv[:, kc, col0:col0 + ncols], rhs=srcT[:, kc, tk],
                                   start=(kc == 0), stop=(kc == KC - 1))
                return ins
            P.add("pe", mm, R=[Bwv] + list(BsrcT), W=[Bpt])
            return pt, Bpt

        def wl(w, c0, ncols=256):
            s_, B_ = wslot()
            v = s_[:, 0:KC * ncols].rearrange("p (k n) -> p k n", k=KC)
            wload(v, w_cols(w, c0, ncols), B_)
            return v, B_

        def front(blk):
            hs, row0 = blk["hs"], blk["row0"]
            tiles = [NT - 1] if blk["pre"] else list(range(NT))
            for t in tiles:
                if blk["pre"]:
                    continue
                P.add("sp", lambda e, t=t: e.dma_start(out=h[hs][t][:], in_=xin[row0 + t * 128: row0 + (t + 1) * 128, :]),
                      W=[Bh[hs][t]], dma=True)
            if blk["first"] and not blk["pre"]:
                P.add("pool", lambda e: e.tensor_copy(out=S[:], in_=S0[:]), R=[BS0], W=[BS])
                P.add("pool", lambda e: e.tensor_copy(out=vT[:, :, 0:32], in_=hist0[:]), R=[Bhist0], W=[Bhist])
            blk["xn"] = {t: norm_part(hs, t, gmix) for t in tiles}

        def front_tr(blk):
            for t in sorted(blk["xn"]):
                tr_part(blk["xn"][t], t, xT, BxT[t])

        def rest(blk, nxt_blk):
            hs, out_row0, preamble = blk["hs"], blk["orow0"], blk["pre"]
            tiles = [NT - 1] if preamble else list(range(NT))
            tk = slice((NT - 1) * 128, T) if preamble else slice(0, T)
            if preamble:
                dbufs = []
                for j in range(NTAP):
                    for c in range(4):
                        b = Buf(f"dg{j}_{c}")
                        dbufs.append(b)
                        P.add("dve", lambda e, j=j, c=c: e.tensor_scalar(
                            out=diag[:, j, c, :], in0=identf[:], scalar1=convwT[:, c, j:j + 1], scalar2=None,
                            op0=ALU.mult), R=[Bconst], W=[b])
                P.add("dve", lambda e: e.memset(st[:, 0, 3:4], 0.0), R=dbufs, W=[Bdiag])

            pt, Bpt = proj_fm(wgl, Bconst, 0, 16, xT, BxT, tk)
            P.add("act", lambda e, pt=pt: e.activation(out=glT[0:16, tk], in_=pt[0:16, tk], func=AF.Copy),
                  R=[Bpt], W=[BglT])
            if "wgv" in blk:
                wg, wv = blk["wgv"]
            else:
                wg = [wl(w_in, 512 + 256 * i) for i in range(2)]
                wv = [wl(w_in, 256 * i) for i in range(2)]
            def b2_chunk(c):
                pg, Bpg = proj_fm(wg[c // 2][0], wg[c // 2][1], (c % 2) * 128, 128, xT, BxT, tk)
                tsg, Btsg = gtmp()
                P.add("act", lambda e: e.activation(out=tsg[:, tk], in_=pg[:, tk], func=AF.Sigmoid), R=[Bpg], W=[Btsg])
                pv, Bpv = proj_fm(wv[c // 2][0], wv[c // 2][1], (c % 2) * 128, 128, xT, BxT, tk)
                P.add("dve", lambda e: e.tensor_tensor(out=vT[:, c, 32 + tk.start:32 + tk.stop], in0=pv[:, tk],
                                                       in1=tsg[:, tk], op=ALU.mult),
                      R=[Bpv, Btsg], W=[BvT[c]])

            zst = {}

            def z_tile(t):
                tsl = slice(t * 128, (t + 1) * 128)
                pz, Bpz = psum()
                P.add("pe", lambda e: e.matmul(pz[:, 0:256], lhsT=glT[0:32, tsl], rhs=wg2[:, :], start=True, stop=True),
                      R=[BglT, Bconst], W=[Bpz])
                te, Bte = gtmp()
                P.add("act", lambda e: e.activation(out=te[:, 0:256], in_=pz[:, 0:256], func=AF.Exp, scale=-1.0),
                      R=[Bpz], W=[Bte])
                tsp, Btsp = gtmp()
                P.add("act", lambda e: e.activation(out=tsp[:, 0:256], in_=te[:, 0:256], func=AF.Ln, bias=1.0),
                      R=[Bte], W=[Btsp])
                zst[t] = (tsp, Btsp)

            def decay_tile(t):
                tsl = slice(t * 128, (t + 1) * 128)
                tsp, Btsp = zst[t]
                pr, Bpr = psum()
                P.add("pe", lambda e: e.matmul(pr[:, 0:256], lhsT=urev[:], rhs=tsp[:, 0:256], start=True, stop=True),
                      R=[Btsp, Bconst], W=[Bpr])
                P.add("act", lambda e: e.activation(out=exprev[:, t, :], in_=pr[:, 0:256], func=AF.Exp),
                      R=[Bpr], W=[Bexprev[t]])
                pc, Bpc = psum()

                def mmc(e):
                    ins = None
                    for hp in range(2):
                        ins = e.matmul(pc[:, hp * 128:(hp + 1) * 128], lhsT=tsp[:, hp * 128:(hp + 1) * 128],
                                       rhs=ucum[:], start=True, stop=True)
                    return ins
                P.add("pe", mmc, R=[Btsp, Bconst], W=[Bpc])
                P.add("act", lambda e: e.activation(
                    out=Eb[:, :, tsl], in_=pc[:, 0:256].rearrange("p (a n) -> p a n", a=2), func=AF.Exp),
                    R=[Bpc], W=[BE[t]])
                if not preamble:
                    P.add("act", lambda e: e.activation(
                        out=Einv[:, :, tsl], in_=pc[:, 0:256].rearrange("p (a n) -> p a n", a=2), func=AF.Exp,
                        scale=-1.0), R=[Bpc], W=[BEi[t]])

            if preamble:
                z_tile(NT - 1)
                for c in range(4):
                    b2_chunk(c)
                decay_tile(NT - 1)
            else:
                b2_chunk(0)
                z_tile(0)
                b2_chunk(1)
                z_tile(1)
                decay_tile(0)
                b2_chunk(2)
                z_tile(2)
                decay_tile(1)
                b2_chunk(3)
                z_tile(3)
                decay_tile(2)

            wk, Bwk = blk["wk"] if "wk" in blk else wl(w_in, 1280)
            if not preamble:
                wq, Bwq = wl(w_in, 1024)
                qk = {}
                for hp in range(2):
                    qk[hp] = (proj_fm(wq, Bwq, hp * 128, 128, xT, BxT), proj_fm(wk, Bwk, hp * 128, 128, xT, BxT))
                    if hp == 0:
                        decay_tile(3)
                    (pq, Bpq), (pk, Bpk) = qk[hp]
                    P.add("dve", lambda e, pq=pq, hp=hp: e.scalar_tensor_tensor(
                        out=q_in[:, hp, :], in0=pq[:], scalar=0.125, in1=Eb[:, hp, :], op0=ALU.mult, op1=ALU.mult),
                        R=[Bpq] + BE, W=[Bq[hp]])
                    P.add("dve", lambda e, pk=pk, hp=hp: e.tensor_tensor(
                        out=k_in[:, hp, :], in0=pk[:], in1=Einv[:, hp, :], op=ALU.mult),
                        R=[Bpk] + BEi, W=[Bk[hp]])

            if not preamble:
                p1, Bp1 = psum()
                p2, Bp2 = psum()
                cst = {}

                def conv_ops(c):
                    pcv, Bpcv = psum()

                    def mmconv(e, pcv=pcv, c=c):
                        ins = None
                        for j in range(NTAP):
                            ins = e.matmul(pcv[:], lhsT=diag[:, j, c, :], rhs=vT[:, c, 2 + j:2 + j + T],
                                           start=(j == 0), stop=(j == NTAP - 1))
                        return ins
                    P.add("pe", mmconv, R=[BvT[c], Bhist, Bdiag], W=[Bpcv])
                    P.add("act", lambda e: e.activation(out=ycv[:, c, :], in_=pcv[:], func=AF.Identity,
                                                        bias=cvec[:, 0, c:c + 1]),
                          R=[Bpcv, Bconst], W=Bycv[c])
                    iq = nxt("ysq", 2)
                    P.add("act", lambda e: e.activation(out=ysq[iq][:].bitcast(BF16)[:, 0:T], in_=pcv[:], func=AF.Square,
                                                        bias=cvec[:, 0, c:c + 1]),
                          R=[Bpcv, Bconst], W=Bysq[iq])
                    cst[c] = iq

                def ones_ops(c):
                    iq = cst[c]
                    P.add("pe", lambda e: e.matmul(p1[:], lhsT=ones[:], rhs=ycv[:, c, :], start=(c == 0), stop=(c == 3)),
                          R=Bycv[c] + [Bconst], W=[Bp1])
                    P.add("pe", lambda e: e.matmul(p2[:], lhsT=ones_bf[:], rhs=ysq[iq][:].bitcast(BF16)[:, 0:T],
                                                   start=(c == 0), stop=(c == 3)),
                          R=Bysq[iq] + [Bconst], W=[Bp2])

                conv_ops(0)
                conv_ops(1)
                ones_ops(0)
                conv_ops(2)
                ones_ops(1)
                conv_ops(3)
                ones_ops(2)
                ones_ops(3)
                P.add("dve", lambda e: e.tensor_scalar(out=mu[:], in0=p1[:], scalar1=1.0 / CC, scalar2=None, op0=ALU.mult),
                      R=[Bp1], W=Bmu)
                P.add("act", lambda e: e.activation(out=rsd[:], in_=p1[:], func=AF.Square, scale=1.0 / CC),
                      R=[Bp1], W=Brsd)
                P.add("dve", lambda e: e.scalar_tensor_tensor(out=rsd[:], in0=p2[:], scalar=1.0 / CC, in1=rsd[:],
                                                              op0=ALU.mult, op1=ALU.subtract), R=[Bp2] + Brsd, W=Brsd)

                def conv_normalize():
                    P.add("act", lambda e: e.activation(out=rsd[:], in_=rsd[:], func=AF.Ln, bias=LN_EPS), R=Brsd, W=Brsd)
                    P.add("act", lambda e: e.activation(out=rsd[:], in_=rsd[:], func=AF.Exp, scale=-0.5), R=Brsd, W=Brsd)
                    for c in range(4):
                        t1, Bt1 = gtmp()
                        P.add("dve", lambda e, t1=t1, c=c: e.tensor_tensor(out=t1[:], in0=ycv[:, c, :], in1=mu[:],
                                                                           op=ALU.subtract), R=Bycv[c] + Bmu, W=[Bt1])
                        P.add("dve", lambda e, t1=t1: e.tensor_tensor(out=t1[:], in0=t1[:], in1=rsd[:], op=ALU.mult),
                              R=[Bt1] + Brsd, W=[Bt1])
                        P.add("act", lambda e, t1=t1, c=c: e.activation(out=yT[:, c, :], in_=t1[:], func=AF.Silu,
                                                                        bias=cvec[:, 2, c:c + 1], scale=cvec[:, 1, c:c + 1]),
                              R=[Bt1, Bconst], W=[ByTc[c]])
                conv_normalize()

            wvt = [blk["wvt0"] if "wvt0" in blk else wl(w_in, 1536), wl(w_in, 1792)]
            for t in tiles:
                tsl = slice(t * 128, (t + 1) * 128)
                pv, Bpv = psum()

                def mmv(e, pv=pv, tsl=tsl):
                    ins = None
                    for hf in range(2):
                        for kc in range(KC):
                            ins = e.matmul(pv[:, hf * 256:(hf + 1) * 256], lhsT=xT[:, kc, tsl], rhs=wvt[hf][0][:, kc, :],
                                           start=(kc == 0), stop=(kc == KC - 1))
                    return ins
                P.add("pe", mmv, R=[wvt[0][1], wvt[1][1], BxT[t]], W=[Bpv])
                P.add("act", lambda e, pv=pv, t=t: e.activation(out=v_tm[:, t, :], in_=pv[:], func=AF.Copy),
                      R=[Bpv], W=[Bvtm[t]])
                pk, Bpk = psum()

                def mmk(e, pk=pk, tsl=tsl):
                    ins = None
                    for kc in range(KC):
                        ins = e.matmul(pk[:, 0:256], lhsT=xT[:, kc, tsl], rhs=wk[:, kc, :], start=(kc == 0),
                                       stop=(kc == KC - 1))
                    return ins
                P.add("pe", mmk, R=[Bwk, BxT[t]], W=[Bpk])
                ko = k_out[:, t, :].rearrange("p (a b n) -> p a b n", a=2, b=2)
                pk4 = pk[:, 0:256].rearrange("p (a b n) -> p a b n", a=2, b=2)
                er4 = exprev[:, t, :].rearrange("p (a b n) -> p a b n", a=2, b=2)
                for hh in range(2):
                    P.add("dve", lambda e, ko=ko, pk4=pk4, er4=er4, hh=hh: e.tensor_tensor(
                        out=ko[:, :, hh, hh * 64:hh * 64 + 64], in0=pk4[:, :, hh, :], in1=er4[:, :, hh, :],
                        op=ALU.mult), R=[Bpk, Bexprev[t]], W=[Bkout[t]])

            if not preamble:
                wgg = [wl(w_in, 2048 + 256 * i) for i in range(2)]
                for hh in range(4):
                    pg, Bpg = proj_fm(wgg[hh // 2][0], wgg[hh // 2][1], (hh % 2) * 128, 128, xT, BxT)
                    P.add("act", lambda e, pg=pg, hh=hh: e.activation(out=sgT[:, hh, :], in_=pg[:], func=AF.Silu),
                          R=[Bpg], W=[BsgT[hh]])

            scs = {}
            def d2_scores(t):
                tsl = slice(t * 128, (t + 1) * 128)
                pscs = [psum(), psum()]

                def mmsc(e, pscs=pscs, tsl=tsl):
                    ins = None
                    for hh in range(2):
                        for hp in range(2):
                            ins = e.matmul(pscs[hh][0][:, hp * 128:(hp + 1) * 128],
                                           lhsT=k_in[hh * 64:(hh + 1) * 64, hp, tsl],
                                           rhs=q_in[hh * 64:(hh + 1) * 64, hp, tsl], start=True, stop=True)
                    return ins
                P.add("pe", mmsc, R=Bq + Bk, W=[pscs[0][1], pscs[1][1]])
                isc = nxt("sc", 2)
                scv = sc_bf[isc][:].rearrange("p (a b) n -> p a b n", b=2)
                for hh in range(2):
                    P.add("dve", lambda e, pscs=pscs, scv=scv, hh=hh: e.tensor_tensor(
                        out=scv[:, :, hh, :], in0=pscs[hh][0][:, 0:256].rearrange("p (a n) -> p a n", a=2),
                        in1=mask[:].unsqueeze(1).to_broadcast([128, 2, 128]), op=ALU.mult),
                        R=[pscs[hh][1], Bconst], W=[Bsc[isc]])
                return isc

            if not preamble:
                scs[0] = d2_scores(0)
                scs[1] = d2_scores(1)
            kvb = [psum() for _ in range(4)]

            def kv_ap(t, n, hp):
                return kvb[n * 2 + t // 2][0][:, (t % 2) * 256 + hp * 128:(t % 2) * 256 + (hp + 1) * 128]

            for n in range(2):
                def mmkv(e, n=n):
                    ins = None
                    rows = slice(n * 64, (n + 1) * 64)
                    for t in tiles:
                        for hp in range(2):
                            for hh in range(2):
                                hd = 2 * hp + hh
                                ins = e.matmul(kv_ap(t, n, hp), lhsT=k_out[rows, t, hd * 128:(hd + 1) * 128],
                                               rhs=v_tm[rows, t, hd * 128:(hd + 1) * 128], start=(hh == 0), stop=(hh == 1))
                    return ins
                P.add("pe", mmkv, R=Bkout + Bvtm, W=[kvb[n * 2][1], kvb[n * 2 + 1][1]])
            for m in range(2 * tiles[0], 2 * NT):
                t, n = divmod(m, 2)
                if not preamble:
                    P.add("dve", lambda e, m=m: e.tensor_copy(out=S_snap[:, m, :, :], in_=S[:]),
                          R=[BS], W=[Bsnap[m]])
                lastc = t * 128 + n * 64 + 63
                for hp in range(2):
                    P.add("dve", lambda e, t=t, n=n, hp=hp, lastc=lastc: e.scalar_tensor_tensor(
                        out=S[:, hp, :], in0=S[:, hp, :], scalar=Eb[:, hp, lastc:lastc + 1],
                        in1=kv_ap(t, n, hp), op0=ALU.mult, op1=ALU.add),
                        R=[BS, BE[t], kvb[n * 2 + t // 2][1]], W=[BS])

            if not preamble:
                wo = [wl(w_out, 256 * i) for i in range(4)]

                def wout_tile(t):
                    tsl = slice(t * 128, (t + 1) * 128)
                    for cb in range(2):
                        pt, Bpt = psum()

                        def mmo2(e, pt=pt, tsl=tsl, cb=cb):
                            ins = None
                            for hf in range(2):
                                for kc in range(KC):
                                    ins = e.matmul(pt[:, hf * 256:(hf + 1) * 256], lhsT=yT[:, kc, tsl],
                                                   rhs=wo[2 * cb + hf][0][:, kc, :], start=(kc == 0), stop=(kc == KC - 1))
                            return ins
                        P.add("pe", mmo2, R=[wo[2 * cb][1], wo[2 * cb + 1][1], ByTg[t]] + ByTc, W=[Bpt])
                        P.add("dve", lambda e, pt=pt, t=t, cb=cb: e.tensor_tensor(
                            out=h[hs][t][:, cb * 512:(cb + 1) * 512], in0=h[hs][t][:, cb * 512:(cb + 1) * 512], in1=pt[:],
                            op=ALU.add), R=[Bpt, Bh[hs][t]], W=[Bh[hs][t]])


                def d2_out(t, isc):
                    tsl = slice(t * 128, (t + 1) * 128)
                    po, Bpo = psum()
                    for n in range(2):
                        m = 2 * t + n

                        def mmo(e, po=po, isc=isc, n=n, t=t, m=m):
                            ins = None
                            for hd in range(4):
                                hp, hh = hd // 2, hd % 2
                                oc = slice(hd * 128 + n * 64, hd * 128 + n * 64 + 64)
                                e.matmul(po[:, oc], lhsT=v_tm[:, t, hd * 128:(hd + 1) * 128],
                                         rhs=sc_bf[isc][:, hd, n * 64:(n + 1) * 64], start=True, stop=False)
                                ins = e.matmul(po[:, oc], lhsT=S_snap[hh * 64:(hh + 1) * 64, m, hp, :],
                                               rhs=q_in[hh * 64:(hh + 1) * 64, hp, t * 128 + n * 64: t * 128 + n * 64 + 64],
                                               start=False, stop=True)
                            return ins
                        P.add("pe", mmo, R=[Bvtm[t], Bsc[isc], Bsnap[m]] + Bq, W=[Bpo])
                    return po, Bpo

                def d2_norm(t, po, Bpo):
                    tsl = slice(t * 128, (t + 1) * 128)
                    tq, Btq = gtmp()
                    P.add("act", lambda e, po=po, tq=tq: e.activation(out=tq[:].bitcast(BF16)[:, 0:T], in_=po[:], func=AF.Square),
                          R=[Bpo], W=[Btq])
                    pss, Bpss = psum()
                    P.add("pe", lambda e, pss=pss, tq=tq: e.matmul(pss[:], lhsT=ones_bf[:], rhs=tq[:].bitcast(BF16)[:, 0:T],
                                                                   start=True, stop=True),
                          R=[Btq, Bconst], W=[Bpss])
                    tr_, Btr = gtmp()
                    P.add("act", lambda e, pss=pss, tr_=tr_: e.activation(out=tr_[:], in_=pss[:], func=AF.Ln,
                                                                          bias=RMS_EPS, scale=1.0 / 128),
                          R=[Bpss], W=[Btr])
                    P.add("act", lambda e, tr_=tr_: e.activation(out=tr_[:], in_=tr_[:], func=AF.Exp, scale=-0.5),
                          R=[Btr], W=[Btr])
                    P.add("dve", lambda e, po=po, tr_=tr_: e.tensor_tensor(out=tr_[:], in0=po[:], in1=tr_[:], op=ALU.mult),
                          R=[Bpo, Btr], W=[Btr])
                    P.add("dve", lambda e, tr_=tr_, tsl=tsl: e.scalar_tensor_tensor(
                        out=yT[:, 4:8, tsl], in0=tr_[:].rearrange("p (a n) -> p a n", a=4), scalar=gng[:, 0:1],
                        in1=sgT[:, :, tsl], op0=ALU.mult, op1=ALU.mult),
                        R=[Btr, Bconst] + BsgT, W=[ByTg[t]])


                def nt(t):
                    norm_transpose(hs, t, gffn, xT, BxT[t])

                o0 = d2_out(0, scs[0])
                scs[2] = d2_scores(2)
                o1 = d2_out(1, scs[1])
                d2_norm(0, *o0)
                scs[3] = d2_scores(3)
                o2 = d2_out(2, scs[2])
                d2_norm(1, *o1)
                wout_tile(0)
                o3 = d2_out(3, scs[3])
                d2_norm(2, *o2)
                wout_tile(1)
                nt(0)
                d2_norm(3, *o3)
                wout_tile(2)
                nt(1)
                wout_tile(3)
                nt(2)
                nt(3)

            P.add("pool", lambda e: e.tensor_copy(out=vT[:, :, 0:32], in_=vT[:, :, T:T + 32]), R=BvT, W=[Bhist])
            if preamble:
                P.add("pool", lambda e: e.tensor_copy(out=S0[:], in_=S[:]), R=[BS], W=[BS0])
                P.add("pool", lambda e: e.tensor_copy(out=hist0[:], in_=vT[:, :, 0:32]), R=[Bhist], W=[Bhist0])
                if nxt_blk is not None:
                    front(nxt_blk)
                    front_tr(nxt_blk)
                return

            if nxt_blk is not None:
                front(nxt_blk)
            for j0 in range(0, NJ, 2):
                wgv, Bwg_ = wl(w_fg, j0 * 128)
                wuv, Bwu_ = wl(w_fu, j0 * 128)
                for jj in range(2):
                    j = j0 + jj
                    pg, Bpg = proj_fm(wgv, Bwg_, jj * 128, 128, xT, BxT)
                    pu, Bpu = proj_fm(wuv, Bwu_, jj * 128, 128, xT, BxT)
                    ts_, Bts = gtmp()
                    P.add("act", lambda e, pg=pg, ts_=ts_: e.activation(out=ts_[:], in_=pg[:], func=AF.Silu),
                          R=[Bpg], W=[Bts])
                    P.add("dve", lambda e, pu=pu, ts_=ts_, j=j: e.tensor_tensor(out=actT[:, j, :], in0=pu[:], in1=ts_[:],
                                                                               op=ALU.mult), R=[Bpu, Bts], W=[Bact[j]])
            if nxt_blk is not None:
                front_tr(nxt_blk)
            for cb in range(2):
                acc = [psum() for _ in range(NT)]
                for j0 in range(0, NJ, 2):
                    wd_, Bwd = wslot()
                    wd = wd_[:, 0:1024].rearrange("p (j n) -> p j n", j=2)
                    wload(wd, w_fd[j0 * 128:(j0 + 2) * 128, cb * 512:(cb + 1) * 512].rearrange("(j p) n -> p j n", p=128), Bwd)

                    def mmd(e, wd=wd, j0=j0, acc=acc):
                        ins = None
                        for jj in range(2):
                            j = j0 + jj
                            for t in range(NT):
                                ins = e.matmul(acc[t][0][:], lhsT=actT[:, j, t * 128:(t + 1) * 128], rhs=wd[:, jj, :],
                                               start=(j == 0), stop=(j == NJ - 1))
                        return ins
                    P.add("pe", mmd, R=[Bwd] + Bact[j0:j0 + 2], W=[a[1] for a in acc])
                    if cb == 1 and j0 == NJ - 2 and nxt_blk is not None:
                        nxt_blk["wgv"] = ([wl(w_in, 512 + 256 * i) for i in range(2)],
                                          [wl(w_in, 256 * i) for i in range(2)])
                for t in range(NT):
                    P.add("dve", lambda e, t=t, cb=cb, a=acc[t][0]: e.tensor_tensor(
                        out=h[hs][t][:, cb * 512:(cb + 1) * 512], in0=h[hs][t][:, cb * 512:(cb + 1) * 512], in1=a[:],
                        op=ALU.add), R=[acc[t][1], Bh[hs][t]], W=[Bh[hs][t]])
            for t in range(NT):
                tj, Btj = gtmp()
                ht, Bht = h[hs][t], Bh[hs][t]
                P.add("act", lambda e, t=t, tj=tj, ht=ht: e.activation(out=tj[:].bitcast(BF16), in_=ht[:], func=AF.Square,
                                                                       accum_out=st[:, t, 4:5]),
                      R=[Bht], W=[Btj, Bst2[t]])
                P.add("act", lambda e, t=t: e.activation(out=st[:, t, 5:6], in_=st[:, t, 4:5], func=AF.Ln,
                                                         bias=RMS_EPS, scale=1.0 / D), R=[Bst2[t]], W=[Bst2[t]])
                P.add("act", lambda e, t=t: e.activation(out=st[:, t, 6:7], in_=st[:, t, 5:6], func=AF.Exp,
                                                         scale=-0.5), R=[Bst2[t]], W=[Bst2[t]])
                P.add("dve", lambda e, t=t, ht=ht: e.scalar_tensor_tensor(out=ht[:], in0=ht[:], scalar=st[:, t, 6:7],
                                                                          in1=gfin[:], op0=ALU.mult, op1=ALU.mult),
                      R=[Bht, Bst2[t], Bconst], W=[Bht])
                P.add("sp", lambda e, t=t, ht=ht: e.dma_start(out=out[out_row0 + t * 128: out_row0 + (t + 1) * 128, :],
                                                              in_=ht[:]),
                      R=[Bht], dma=True, dkey=Bht.name)

        blocks = [dict(hs=0, row0=0, orow0=0, first=False, pre=True, **pre_w)]
        for s_i in range(nseq):
            for b in range(nblk_seq):
                r = s_i * seqlen + b * T
                blocks.append(dict(hs=len(blocks) % 2, row0=T + r, orow0=r, first=(b == 0), pre=False))
        front(blocks[0])
        front_tr(blocks[0])
        P.add("dve", lambda e: e.memset(st[:, 0, 7:8], 0.0), R=list(setup_bufs), W=[Bconst])
        for bi, blk in enumerate(blocks):
            rest(blk, blocks[bi + 1] if bi + 1 < len(blocks) else None)
        P.add("sp", None, W=Bh[0] + Bh[1])
        P.finalize()
        dkeys = sorted(P.dcount.keys())
        esem = {e: es.enter_context(nc.semaphore(f"sem_{e}")) for e in Prog.ENGS}
        dsem = {k: es.enter_context(nc.semaphore(f"dsem_{k}")) for k in dkeys}
        with nc.Block() as blk:
            @blk.tensor
            def _(e):
                P.emit("pe", e, esem, dsem)

            @blk.scalar
            def _(e):
                P.emit("act", e, esem, dsem)

            @blk.vector
            def _(e):
                P.emit("dve", e, esem, dsem)

            @blk.gpsimd
            def _(e):
                P.emit("pool", e, esem, dsem)

            @blk.sync
            def _(e):
                P.emit("sp", e, esem, dsem)
    return nc


def make_in_maps(inputs, nseq, seqlen, n_cores):
    x = np.ascontiguousarray(inputs["x"], dtype=np.float32)
    meta = np.asarray(inputs["meta_tokens"], dtype=np.float32)
    pre = np.zeros((T, D), np.float32)
    pre[T - meta.shape[0]:] = meta
    conv_wT = np.ascontiguousarray(
        np.asarray(inputs["conv_w"][0], np.float32).T.reshape(4, 128, NTAP).transpose(1, 0, 2))
    vecs = np.stack([np.asarray(inputs[k][0], np.float32).reshape(4, 128).T
                     for k in ("conv_b", "conv_ln_g", "conv_ln_b")], axis=1)
    common = {
        "w_in": np.ascontiguousarray(inputs["w_in"][0], dtype=np.float32),
        "w_out": np.ascontiguousarray(inputs["w_out"][0], dtype=np.float32),
        "w_ffn_gate": np.ascontiguousarray(inputs["w_ffn_gate"][0], dtype=np.float32),
        "w_ffn_up": np.ascontiguousarray(inputs["w_ffn_up"][0], dtype=np.float32),
        "w_ffn_down": np.ascontiguousarray(inputs["w_ffn_down"][0], dtype=np.float32),
        "norm_mix_g": np.asarray(inputs["norm_mix_g"], np.float32).reshape(1, D),
        "norm_ffn_g": np.asarray(inputs["norm_ffn_g"], np.float32).reshape(1, D),
        "norm_final_g": np.asarray(inputs["norm_final_g"], np.float32).reshape(1, D),
        "conv_wT": conv_wT,
        "conv_vecs": np.ascontiguousarray(vecs),
        "gla_w_gate2": np.ascontiguousarray(inputs["gla_w_gate2"][0], dtype=np.float32),
        "gla_gate_b": np.asarray(inputs["gla_gate_b"], np.float32).reshape(1, 256),
        "gla_norm_g": np.asarray(inputs["gla_norm_g"], np.float32).reshape(128, 1),
    }
    maps = []
    for c in range(n_cores):
        xc = x[c * nseq:(c + 1) * nseq].reshape(nseq * seqlen, D)
        m = dict(common)
        m["xin"] = np.concatenate([pre, xc], axis=0)
        maps.append(m)
    return maps


def kernel(**inputs):
    x = inputs["x"]
    bsz, seqlen, _ = x.shape
    nseq = bsz // N_CORES
    nc = build_program(nseq, seqlen)
    in_maps = make_in_maps(inputs, nseq, seqlen, N_CORES)
    res = run_bass_kernel_spmd(nc, in_maps, core_ids=list(range(N_CORES)))
    outs = [np.asarray(r["out"], dtype=np.float32).reshape(nseq, seqlen, D) for r in res.results]
    return np.concatenate(outs, axis=0)
```
